# Optimizing a Trainium2 kernel written in Bass

```python
import jax, jax.numpy as jnp
from jax import lax
import numpy as np

D_MODEL = 1024
BATCH = 8
SEQ = 2048
DEPTH = 2

GRID_W = 64
CTX_LEN = 256
D_MIX = D_MODEL
MLA_HEADS = 8
MLA_NOPE = 64
MLA_ROPE = 32
MLA_V = 64
MLA_QK = MLA_NOPE + MLA_ROPE
MLA_WIDTH = MLA_HEADS * MLA_V
Q_LORA = 256
KV_LORA = 128
Q_BLOCK = 128
RET_HEADS = 4
RET_DK = 64
RET_DV = 64
RET_WIDTH = RET_HEADS * RET_DV
RET_CHUNK = 128
CONV_WIDTH = D_MIX - MLA_WIDTH - RET_WIDTH
CONV_K = 31

ROPE_BASE = 10000.0
EPS = 1e-5
ALPHA = (2 * DEPTH) ** 0.25
BETA = (8 * DEPTH) ** -0.25

IN_SIZES = (Q_LORA, KV_LORA, MLA_ROPE, MLA_WIDTH,
            RET_HEADS * RET_DK, RET_HEADS * RET_DK, RET_WIDTH, RET_WIDTH,
            2 * CONV_WIDTH, CONV_WIDTH)
IN_COLS = sum(IN_SIZES)

kernel_name = 'hybrid_mla_retention_conformer_dit'


def _split_cols(p):
    offs, s = [], 0
    for n in IN_SIZES[:-1]:
        s += n
        offs.append(s)
    return jnp.split(p, offs, axis=-1)


def _standardize(x):
    xf = x.astype(jnp.float32)
    mu = jnp.mean(xf, axis=-1, keepdims=True)
    var = jnp.mean(jnp.square(xf - mu), axis=-1, keepdims=True)
    return ((xf - mu) * lax.rsqrt(var + EPS)).astype(x.dtype)


def _layer_norm(x, g, b):
    xf = x.astype(jnp.float32)
    mu = jnp.mean(xf, axis=-1, keepdims=True)
    var = jnp.mean(jnp.square(xf - mu), axis=-1, keepdims=True)
    y = (xf - mu) * lax.rsqrt(var + EPS) * g.astype(jnp.float32) + b.astype(jnp.float32)
    return y.astype(x.dtype)


def _rms_norm(x, g):
    xf = x.astype(jnp.float32)
    y = xf * lax.rsqrt(jnp.mean(jnp.square(xf), axis=-1, keepdims=True) + EPS) * g.astype(jnp.float32)
    return y.astype(x.dtype)


def _rotate(x, pos):
    d2 = x.shape[-1]
    inv = ROPE_BASE ** (-jnp.arange(0, d2, 2, dtype=jnp.float32) / d2)
    ang = pos.astype(jnp.float32)[:, None] * inv[None, :]
    cos = jnp.cos(ang)[:, None, :].astype(x.dtype)
    sin = jnp.sin(ang)[:, None, :].astype(x.dtype)
    x1, x2 = jnp.split(x, 2, axis=-1)
    return jnp.concatenate([x1 * cos - x2 * sin, x1 * sin + x2 * cos], axis=-1)


def _axial_rope(x, row, col):
    half = x.shape[-1] // 2
    return jnp.concatenate([_rotate(x[..., :half], row), _rotate(x[..., half:], col)], axis=-1)


def _block_attention(q, k, v):
    B, L, H, dqk = q.shape
    nb = L // Q_BLOCK
    scale = dqk ** -0.5
    qb = q.reshape(B, nb, Q_BLOCK, H, dqk).transpose(1, 0, 2, 3, 4)

    def one_block(qi):
        s = jnp.einsum('bqhd,bkhd->bhqk', qi, k).astype(jnp.float32) * scale
        p = jax.nn.softmax(s, axis=-1).astype(v.dtype)
        return jnp.einsum('bhqk,bkhe->bqhe', p, v)

    o = lax.map(one_block, qb)
    return o.transpose(1, 0, 2, 3, 4).reshape(B, L, H * v.shape[-1])


def _mla_q(p_q, g_q, w_uq, row, col):
    B, L, _ = p_q.shape
    q = (_rms_norm(p_q, g_q) @ w_uq).reshape(B, L, MLA_HEADS, MLA_QK)
    q_nope, q_rope = q[..., :MLA_NOPE], q[..., MLA_NOPE:]
    if row is not None:
        q_rope = _axial_rope(q_rope, row, col)
    return jnp.concatenate([q_nope, q_rope], axis=-1)


def _mla_kv(p_kv, p_kr, g_kv, w_ukv, row, col):
    B, L, _ = p_kv.shape
    kv = (_rms_norm(p_kv, g_kv) @ w_ukv).reshape(B, L, MLA_HEADS, MLA_NOPE + MLA_V)
    k_nope, v = kv[..., :MLA_NOPE], kv[..., MLA_NOPE:]
    k_rope = p_kr[:, :, None, :]
    if row is not None:
        k_rope = _axial_rope(k_rope, row, col)
    k = jnp.concatenate([k_nope, jnp.broadcast_to(k_rope, (B, L, MLA_HEADS, MLA_ROPE))], axis=-1)
    return k, v


def _ret_qkv(p_q, p_k, p_v, row, col):
    B, L, _ = p_q.shape
    q = p_q.reshape(B, L, RET_HEADS, RET_DK)
    k = p_k.reshape(B, L, RET_HEADS, RET_DK) * (RET_DK ** -0.5)
    v = p_v.reshape(B, L, RET_HEADS, RET_DV)
    if row is not None:
        q = _axial_rope(q, row, col)
        k = _axial_rope(k, row, col)
    to_bhld = lambda t: t.transpose(0, 2, 1, 3).astype(jnp.float32)
    return to_bhld(q), to_bhld(k), to_bhld(v)


def _retention_chunks(q, k, v, log_gamma, s0, strict):
    B, H, L, dk = q.shape
    dv = v.shape[-1]
    n = L // RET_CHUNK
    idx = jnp.arange(RET_CHUNK, dtype=jnp.float32)
    diff = idx[:, None] - idx[None, :]
    lg = log_gamma[:, None, None]
    mask = (diff > 0) if strict else (diff >= 0)
    dmat = jnp.where(mask, jnp.exp(lg * jnp.maximum(diff, 0.0)), 0.0)
    q_dec = jnp.exp(lg * (idx[None, :, None] + 1.0))
    k_dec = jnp.exp(lg * (RET_CHUNK - 1.0 - idx)[None, :, None])
    c_dec = jnp.exp(log_gamma * RET_CHUNK)[:, None, None]

    def chunked(t):
        return t.reshape(B, H, n, RET_CHUNK, t.shape[-1]).transpose(2, 0, 1, 3, 4)

    def step(state, qkv):
        qi, ki, vi = qkv
        inner = jnp.einsum('bhij,bhje->bhie', jnp.einsum('bhid,bhjd->bhij', qi, ki) * dmat, vi)
        cross = jnp.einsum('bhid,bhde->bhie', qi * q_dec, state)
        state = state * c_dec + jnp.einsum('bhjd,bhje->bhde', ki * k_dec, vi)
        return state, inner + cross

    s_fin, o = lax.scan(step, s0, (chunked(q), chunked(k), chunked(v)))
    return o.transpose(1, 2, 0, 3, 4).reshape(B, H, L, dv), s_fin


def _ret_out(o, gate, g, b):
    B, H, L, dv = o.shape
    of = o.transpose(0, 2, 1, 3)
    mu = jnp.mean(of, axis=-1, keepdims=True)
    var = jnp.mean(jnp.square(of - mu), axis=-1, keepdims=True)
    y = ((of - mu) * lax.rsqrt(var + EPS)).reshape(B, L, H * dv)
    y = y * g.astype(jnp.float32) + b.astype(jnp.float32)
    return y.astype(gate.dtype) * jax.nn.silu(gate)


def _conformer_conv(p_glu, p_gate, dw, dw_b, ln_g, ln_b, pw, pw_b):
    a, g = jnp.split(p_glu, 2, axis=-1)
    u = a * jax.nn.sigmoid(g)
    y = lax.conv_general_dilated(u, dw[:, None, :], window_strides=(1,),
                                 padding=[(CONV_K // 2, CONV_K // 2)],
                                 dimension_numbers=('NWC', 'WIO', 'NWC'),
                                 feature_group_count=CONV_WIDTH) + dw_b
    y = jax.nn.silu(_layer_norm(y, ln_g, ln_b))
    y = y @ pw + pw_b
    return y * jax.nn.silu(p_gate)


def _layer(x, hc, c, c_ctx, w_mod, b_mod, w_in, g_q, w_uq, g_kv, w_ukv, dec_f, dec_b, gn_g, gn_b,
           dw, dw_b, cln_g, cln_b, pw, pw_b, w_out, ln_g, ln_b, row, col, need_ctx):
    B = x.shape[0]
    shift, scale, gate = jnp.split(jax.nn.silu(c) @ w_mod + b_mod, 3, axis=-1)
    shift_c, scale_c, gate_c = jnp.split(jax.nn.silu(c_ctx) @ w_mod + b_mod, 3, axis=-1)
    u = _standardize(x) * (1 + scale[:, None]) + shift[:, None]
    uc = _standardize(hc) * (1 + scale_c) + shift_c
    (pq, pkv, pkr, pg_mla, prq, prk, prv, pg_ret, pglu, pg_conv) = _split_cols(u @ w_in)
    (cq, ckv, ckr, cg_mla, crq, crk, crv, cg_ret, cglu, cg_conv) = _split_cols(uc @ w_in)

    kc, vc = _mla_kv(ckv, ckr, g_kv, w_ukv, None, None)
    k, v = _mla_kv(pkv, pkr, g_kv, w_ukv, row, col)
    q = _mla_q(pq, g_q, w_uq, row, col)
    o_mla = _block_attention(q, jnp.concatenate([kc, k], axis=1),
                             jnp.concatenate([vc, v], axis=1)) * jax.nn.silu(pg_mla)

    qc_r, kc_r, vc_r = _ret_qkv(crq, crk, crv, None, None)
    q_r, k_r, v_r = _ret_qkv(prq, prk, prv, row, col)
    lg_f = jax.nn.log_sigmoid(dec_f.astype(jnp.float32))
    lg_b = jax.nn.log_sigmoid(dec_b.astype(jnp.float32))
    zero = jnp.zeros((B, RET_HEADS, RET_DK, RET_DV), jnp.float32)
    fl = lambda t: jnp.flip(t, axis=2)
    oc_f, sc_f = _retention_chunks(qc_r, kc_r, vc_r, lg_f, zero, False)
    oc_b, sc_b = _retention_chunks(fl(qc_r), fl(kc_r), fl(vc_r), lg_b, zero, True)
    o_f, _ = _retention_chunks(q_r, k_r, v_r, lg_f, sc_f, False)
    o_b, _ = _retention_chunks(fl(q_r), fl(k_r), fl(v_r), lg_b, sc_b, True)
    o_ret = _ret_out(o_f + fl(o_b), pg_ret, gn_g, gn_b)

    o_conv = _conformer_conv(pglu, pg_conv, dw, dw_b, cln_g, cln_b, pw, pw_b)

    y = jnp.concatenate([o_mla, o_ret, o_conv], axis=-1) @ w_out
    x_new = _layer_norm(ALPHA * x + gate[:, None] * y, ln_g, ln_b)
    if not need_ctx:
        return x_new, None

    qc = _mla_q(cq, g_q, w_uq, None, None)
    oc_mla = _block_attention(qc, kc, vc) * jax.nn.silu(cg_mla)
    oc_ret = _ret_out(oc_f + fl(oc_b), cg_ret, gn_g, gn_b)
    oc_conv = _conformer_conv(cglu, cg_conv, dw, dw_b, cln_g, cln_b, pw, pw_b)
    yc = jnp.concatenate([oc_mla, oc_ret, oc_conv], axis=-1) @ w_out
    hc_new = _layer_norm(ALPHA * hc + gate_c * yc, ln_g, ln_b)
    return x_new, hc_new


def setup_inputs(seed: int = 0) -> dict:
    key = jax.random.key(seed)
    ks = jax.random.split(key, 24)
    f32 = jnp.float32
    nrm = lambda k, shape, s: jax.random.normal(k, shape, f32) * s
    gam = 1.0 - 2.0 ** (-5.0 - jnp.arange(RET_HEADS, dtype=f32))
    logit = jnp.log(gam) - jnp.log1p(-gam)
    return {
        'x': nrm(ks[0], (BATCH, SEQ, D_MODEL), 1.0),
        'c': nrm(ks[1], (BATCH, D_MODEL), 1.0),
        'ctx': nrm(ks[2], (BATCH, CTX_LEN, D_MODEL), 1.0),
        'c_ctx': nrm(ks[3], (D_MODEL,), 1.0),
        'w_mod': nrm(ks[4], (DEPTH, D_MODEL, 3 * D_MODEL), D_MODEL ** -0.5),
        'b_mod': nrm(ks[5], (DEPTH, 3 * D_MODEL), 0.02),
        'w_in': nrm(ks[6], (DEPTH, D_MODEL, IN_COLS), D_MODEL ** -0.5),
        'mla_q_norm': 1.0 + nrm(ks[7], (DEPTH, Q_LORA), 0.02),
        'w_uq': nrm(ks[8], (DEPTH, Q_LORA, MLA_HEADS * MLA_QK), Q_LORA ** -0.5),
        'mla_kv_norm': 1.0 + nrm(ks[9], (DEPTH, KV_LORA), 0.02),
        'w_ukv': nrm(ks[10], (DEPTH, KV_LORA, MLA_HEADS * (MLA_NOPE + MLA_V)), KV_LORA ** -0.5),
        'ret_decay_fwd': logit[None, :] + nrm(ks[11], (DEPTH, RET_HEADS), 0.1),
        'ret_decay_bwd': logit[None, :] + nrm(ks[12], (DEPTH, RET_HEADS), 0.1),
        'ret_gn_g': 1.0 + nrm(ks[13], (DEPTH, RET_WIDTH), 0.02),
        'ret_gn_b': nrm(ks[14], (DEPTH, RET_WIDTH), 0.02),
        'conv_dw': nrm(ks[15], (DEPTH, CONV_K, CONV_WIDTH), CONV_K ** -0.5),
        'conv_dw_b': nrm(ks[16], (DEPTH, CONV_WIDTH), 0.02),
        'conv_ln_g': 1.0 + nrm(ks[17], (DEPTH, CONV_WIDTH), 0.02),
        'conv_ln_b': nrm(ks[18], (DEPTH, CONV_WIDTH), 0.02),
        'conv_pw': nrm(ks[19], (DEPTH, CONV_WIDTH, CONV_WIDTH), CONV_WIDTH ** -0.5),
        'conv_pw_b': nrm(ks[20], (DEPTH, CONV_WIDTH), 0.02),
        'w_out': nrm(ks[21], (DEPTH, D_MIX, D_MODEL), BETA * D_MIX ** -0.5),
        'ln_g': 1.0 + nrm(ks[22], (DEPTH, D_MODEL), 0.02),
        'ln_b': nrm(ks[23], (DEPTH, D_MODEL), 0.02),
    }


def reference(x, c, ctx, c_ctx, w_mod, b_mod, w_in, mla_q_norm, w_uq, mla_kv_norm, w_ukv,
              ret_decay_fwd, ret_decay_bwd, ret_gn_g, ret_gn_b, conv_dw, conv_dw_b, conv_ln_g, conv_ln_b,
              conv_pw, conv_pw_b, w_out, ln_g, ln_b):
    L = x.shape[1]
    ROWS = L // GRID_W
    row = jnp.broadcast_to(jnp.arange(ROWS, dtype=jnp.int32)[:, None], (ROWS, GRID_W)).reshape(-1)
    col = jnp.broadcast_to(jnp.arange(GRID_W, dtype=jnp.int32)[None, :], (ROWS, GRID_W)).reshape(-1)
    hc = ctx
    for l in range(DEPTH):
        x, hc = _layer(x, hc, c, c_ctx, w_mod[l], b_mod[l], w_in[l], mla_q_norm[l], w_uq[l],
                       mla_kv_norm[l], w_ukv[l], ret_decay_fwd[l], ret_decay_bwd[l], ret_gn_g[l], ret_gn_b[l],
                       conv_dw[l], conv_dw_b[l], conv_ln_g[l], conv_ln_b[l], conv_pw[l], conv_pw_b[l],
                       w_out[l], ln_g[l], ln_b[l], row, col, l < DEPTH - 1)
    return x
```

```python
import contextlib
import numpy as np
import concourse.bass as bass
import concourse.mybir as mybir
from concourse.bass_utils import run_bass_kernel_spmd

F32 = mybir.dt.float32
BF16 = mybir.dt.bfloat16
AF = mybir.ActivationFunctionType
ALU = mybir.AluOpType
ENGS = ['pe', 'act', 'dve', 'pool', 'sp']
N_DSEM = 24
SBUF_TRACE = False
import os
CUT = int(os.environ.get('CUT', '0'))

D = 1024; L = 2048; CT = 256; T = CT + L; NT = T // 128; DEPTH = 2
EPS = 1e-5
ALPHA = (2 * DEPTH) ** 0.25
BLOCKS = [(0, 256), (256, 512), (768, 512), (1280, 512), (1792, 512)]
NPP = 24 + 2 + 1 + 2 + 2 + 2 + 2 + 2 + 2 + 62
WIN_EXT = 3264
UOFF_C = 15; UOFF_L = 15 + 256 + 30; UW = UOFF_L + 2048 + 15


class Res:
    __slots__ = ('name', 'w', 'r')

    def __init__(self, name=''):
        self.name = name
        self.w = None
        self.r = {}


class Prog:
    def __init__(self, nc, stack):
        self.nc = nc
        self.ops = {e: [] for e in ENGS}
        self.sem = {e: stack.enter_context(nc.semaphore('tk_' + e)) for e in ENGS}
        self.cnt = {e: 0 for e in ENGS}
        self.seen = {}
        self.dsem = [stack.enter_context(nc.semaphore('dq%d' % i)) for i in range(N_DSEM)]
        self.dcnt = [0] * N_DSEM
        self.dnext = 0
        self.out_tickets = []
        self.pending = {e: [] for e in ENGS}

    def _need(self, eng, t, waits):
        if t is None:
            return
        key, sem, val = t
        if key == eng and eng == 'pe':
            return
        if self.seen.get((eng, key), 0) >= val:
            return
        self.seen[(eng, key)] = val
        waits.append((sem, val))

    def _deps(self, eng, reads, writes):
        waits = self.pending[eng]
        self.pending[eng] = []
        for r in reads:
            self._need(eng, r.w, waits)
        for w in writes:
            self._need(eng, w.w, waits)
            for k, t in w.r.items():
                self._need(eng, t, waits)
        return waits

    def op(self, eng, fn, reads=(), writes=()):
        waits = self._deps(eng, reads, writes)
        self.cnt[eng] += 1
        t = (eng, self.sem[eng], self.cnt[eng])
        self.ops[eng].append((waits, fn, (self.sem[eng], 1)))
        for r in reads:
            r.r[eng] = t
        for w in writes:
            w.w = t
            w.r = {}
        return t

    def dma(self, eng, fn, reads=(), writes=(), is_output=False):
        i = self.dnext
        self.dnext = (self.dnext + 1) % N_DSEM
        waits = self._deps(eng, reads, writes)
        key = 'd%d' % i
        if self.dcnt[i] > 0:
            self._need(eng, (key, self.dsem[i], self.dcnt[i]), waits)
        self.dcnt[i] += 16
        t = (key, self.dsem[i], self.dcnt[i])
        self.ops[eng].append((waits, fn, (self.dsem[i], 16)))
        for r in reads:
            r.r[key] = t
        for w in writes:
            w.w = t
            w.r = {}
        if is_output:
            self.out_tickets.append(t)
        return t

    def barrier(self):
        for e in ENGS:
            for f in ENGS:
                if f != e and self.cnt[f] > 0:
                    self._need(e, (f, self.sem[f], self.cnt[f]), self.pending[e])
            for i in range(N_DSEM):
                if self.dcnt[i] > 0:
                    self._need(e, ('d%d' % i, self.dsem[i], self.dcnt[i]), self.pending[e])

    def emit(self):
        nc = self.nc
        fin = []
        for t in self.out_tickets:
            self._need('sp', t, fin)
        with nc.Block() as block:
            def run(e, engobj):
                for waits, fn, inc in self.ops[e]:
                    for s, v in waits:
                        engobj.wait_ge(s, v)
                    fn(engobj).then_inc(inc[0], inc[1])
                for s, v in self.pending[e]:
                    engobj.wait_ge(s, v)
                if e == 'sp':
                    for s, v in fin:
                        engobj.wait_ge(s, v)

            @block.tensor
            def _(eng):
                run('pe', eng)

            @block.scalar
            def _(eng):
                run('act', eng)

            @block.vector
            def _(eng):
                run('dve', eng)

            @block.gpsimd
            def _(eng):
                run('pool', eng)

            @block.sync
            def _(eng):
                run('sp', eng)


def bc(a, pos, n):
    ap = [list(x) for x in a.ap]
    ap.insert(pos, [0, n])
    return bass.AP(a.tensor, a.offset, ap)


def colbc(a, n):
    return bass.AP(a.tensor, a.offset, [list(a.ap[0]), [0, n]])


def build(n_layers=DEPTH, dbg=None, stage=6):
    nc = bass.Bass("TRN2", target_bir_lowering=False)
    dt_in = lambda n, s: nc.dram_tensor(n, s, F32, kind="ExternalInput")
    xin_d = dt_in("xin", [T, D])
    cc_d = dt_in("cc", [128, 16])
    wmod_d = dt_in("w_mod", [2, 1024, 3072])
    bgate_d = dt_in("b_gate", [2, 1024])
    win_d = dt_in("w_in", [2, 1024, WIN_EXT])
    wuq_d = dt_in("w_uq", [2, 256, 1024])
    wukv_d = dt_in("w_ukv", [2, 128, 1024])
    dec_d = dt_in("dec", [2, 8])
    pw_d = dt_in("pw", [2, 256, 256])
    wout_d = dt_in("w_out", [2, 1024, 1024])
    lng_d = dt_in("ln_g", [2, 1024])
    lnb_d = dt_in("ln_b", [2, 1024])
    pp_d = dt_in("pp", [2, 128, NPP])
    ropeM_d = dt_in("ropeM", [128, T])
    ropeRC_d = dt_in("ropeRC", [128, T])
    ropeRS_d = dt_in("ropeRS", [128, T])
    tab_d = dt_in("tab", [128, 6 * 128 + 2])
    out_d = nc.dram_tensor("out", [L, D], F32, kind="ExternalOutput")
    x1_d = nc.dram_tensor("x1", [T, D], F32, kind="Internal")
    dbg_outs = {}

    with contextlib.ExitStack() as st:
        P = Prog(nc, st)

        uid = [0]

        live = [0, 0]

        def sbt(stack, name, shape, dt):
            uid[0] += 1
            nb = int(np.prod(shape[1:])) * (2 if dt == BF16 else 4)
            nb = (nb + 31) // 32 * 32
            live[0] += nb
            live[1] = max(live[1], live[0])
            stack.callback(lambda: live.__setitem__(0, live[0] - nb))
            if SBUF_TRACE:
                print("alloc", name, nb, "live", live[0])
            return stack.enter_context(nc.sbuf_tensor("sb%d_%s" % (uid[0], name), shape, dt))

        def op(eng, name, *a, reads=(), writes=(), **kw):
            P.op(eng, lambda e: getattr(e, name)(*a, **kw), reads, writes)

        def mm(out, lhsT, rhs, start, stop, reads, writes):
            P.op('pe', lambda e: e.matmul(out, lhsT, rhs, start=start, stop=stop), reads, writes)

        def tp(out, in_, ident, reads, writes):
            P.op('pe', lambda e: e.transpose(out, in_, ident), reads, writes)

        def dma(out, in_, reads=(), writes=(), is_output=False, eng='sp'):
            P.dma(eng, lambda e: e.dma_start(out=out, in_=in_), reads, writes, is_output)

        def dump(name, ap, reads):
            if dbg is None or name not in dbg:
                return
            d = nc.dram_tensor("dbg_" + name, list(ap.shape), ap.dtype, kind="ExternalOutput")
            dbg_outs[name] = d
            dma(d.ap(), ap, reads=reads, is_output=True)

        uT = sbt(st, "uT", [128, 8, T], BF16)
        catT = sbt(st, "catT", [128, 8, T], BF16)
        r_uTd = [Res('uTd%d' % i) for i in range(NT)]
        r_uTa = [Res('uTa%d' % i) for i in range(NT)]
        r_uT = [r for i in range(NT) for r in (r_uTd[i], r_uTa[i])]
        r_cat = [[Res('cat%d_%d' % (c, i)) for i in range(NT)] for c in range(8)]
        wst = [sbt(st, "wst%d" % i, [128, 8, 256], F32) for i in range(2)]
        r_wst = [Res('wst%d' % i) for i in range(2)]
        wbf = [sbt(st, "wbf%d" % i, [128, 8, 256], BF16) for i in range(2)]
        r_wbf = [Res('wbf%d' % i) for i in range(2)]
        wrot = [0]
        NRES = 4
        RES0 = NT - NRES
        xres = sbt(st, "xres", [128, NRES, D], F32)
        r_xres = [Res('xres%d' % i) for i in range(NRES)]
        wgate = [sbt(st, "wgate%d" % i, [128, 8, 256], BF16) for i in range(2)]
        r_wgate = [Res('wgate%d' % i) for i in range(2)]
        ident = sbt(st, "ident", [128, 128], BF16)
        identf = sbt(st, "identf", [128, 128], F32)
        r_ident = Res('ident')
        ones_bf = sbt(st, "ones_bf", [128, 128], BF16)
        ones_f = sbt(st, "ones_f", [128, 128], F32)
        r_ones = Res('ones')
        tab = sbt(st, "tab", [128, 6 * 128 + 2], F32)
        r_tab = Res('tab')
        cT = sbt(st, "cT", [128, 8, 2], F32)
        r_c = Res('c')
        pp = sbt(st, "pp", [128, NPP], F32)
        r_pp = Res('pp')
        modT = sbt(st, "modT", [128, 16, 2], F32)
        r_mod = Res('modT')
        gate = sbt(st, "gate", [128, 2, D], F32)
        r_gate = Res('gate')
        stat = sbt(st, "stat", [128, 4, 16], F32)
        r_stat = [Res('stat%d' % i) for i in range(4)]
        lg = sbt(st, "lg", [128, 8], F32)
        lgt = sbt(st, "lgt", [128, 4, 8], F32)
        r_lg = Res('lg')

        psA = st.enter_context(nc.psum_tensor("psA", [128, 1024], F32))
        r_A = [Res('psA0'), Res('psA1')]
        psS = [st.enter_context(nc.psum_tensor("psS%d" % i, [128, 512], F32)) for i in range(3)]
        r_S = [Res('psS%d' % i) for i in range(3)]
        psG = [st.enter_context(nc.psum_tensor("psG%d" % i, [128, 512], F32)) for i in range(3)]
        r_G = [Res('psG%d' % i) for i in range(3)]
        grot = [0]; srot = [0]

        def nextG():
            i = grot[0]; grot[0] = (i + 1) % 3
            return psG[i], r_G[i]

        def nextS():
            i = srot[0]; srot[0] = (i + 1) % 3
            return psS[i], r_S[i]

        def tiles(t0, n):
            return range(t0 // 128, (t0 + n) // 128)

        def uTres(t0, n):
            return [r for i in tiles(t0, n) for r in (r_uTd[i], r_uTa[i])]

        dma(tab[:], tab_d.ap(), writes=[r_tab])
        dma(cT[:].rearrange("p k j -> p (k j)"), cc_d.ap(), writes=[r_c])
        op('pool', 'memset', identf[:], 0.0, writes=[r_ident])
        P.op('pool', lambda e: e.affine_select(identf[:], identf[:], [[-1, 128]], ALU.not_equal, 1.0, base=0,
                                               channel_multiplier=1), reads=[r_ident], writes=[r_ident])
        op('dve', 'tensor_copy', ident[:], identf[:], reads=[r_ident], writes=[r_ident])
        permT = sbt(st, "permT", [128, 128], BF16)
        r_perm = Res('permT')
        for h_ in range(2):
            for (d0, s0) in ((0, 16), (16, 0), (32, 48), (48, 32)):
                op('dve', 'tensor_copy', permT[:, 64 * h_ + d0:64 * h_ + d0 + 16], ident[:, 64 * h_ + s0:64 * h_ + s0 + 16],
                   reads=[r_ident], writes=[r_perm])
        op('pool', 'memset', ones_bf[:], 1.0, writes=[r_ones])
        op('pool', 'memset', ones_f[:], 1.0, writes=[r_ones])
        op('act', 'activation', cT[:], cT[:], AF.Silu, reads=[r_c], writes=[r_c])
        def load_w(dh, base_off, row_stride, ncols, nk=8, mul_ap=None, mul_reads=(), cast_eng='dve'):
            b = wrot[0]; wrot[0] = 1 - b
            src = bass.AP(dh, base_off, [[row_stride, 128], [128 * row_stride, nk], [1, ncols]])
            dma(wst[b][:, 0:nk, 0:ncols], src, writes=[r_wst[b]])
            if mul_ap is None:
                op(cast_eng, 'tensor_copy', wbf[b][:, 0:nk, 0:ncols], wst[b][:, 0:nk, 0:ncols],
                   reads=[r_wst[b]], writes=[r_wbf[b]])
            else:
                op('dve', 'tensor_tensor', wbf[b][:, 0:nk, 0:ncols], wst[b][:, 0:nk, 0:ncols], mul_ap, ALU.mult,
                   reads=[r_wst[b]] + list(mul_reads), writes=[r_wbf[b]])
            return wbf[b], r_wbf[b]

        def load_w32(dh, base_off, row_stride, ncols):
            b = wrot[0]; wrot[0] = 1 - b
            src = bass.AP(dh, base_off, [[row_stride, 128], [128 * row_stride, 8], [1, ncols]])
            dma(wst[b][:, :, 0:ncols], src, writes=[r_wst[b]])
            return wst[b], r_wst[b]

        def proj_fm(w, rw, c0, M, blocks, consume, nk=8, src=None, src_res=None):
            for (t0, n) in blocks:
                ps, rp = nextG()
                for k in range(nk):
                    if src is None:
                        rhs = uT[:, k, t0:t0 + n]; rr = uTres(t0, n)
                    else:
                        rhs = src[:, k, t0:t0 + n]; rr = src_res(t0, n)
                    mm(ps[0:M, 0:n], w[:, k, c0:c0 + M], rhs, k == 0, k == nk - 1, [rw] + rr, [rp])
                consume(ps, rp, t0, n)

        def rstd_from_var(var_ap, out_ap, rr, tmp_ap):
            op('act', 'activation', tmp_ap, var_ap, AF.Ln, bias=EPS_AP[0:var_ap.shape[0], :], reads=[rr, r_tab], writes=[rr])
            op('act', 'activation', out_ap, tmp_ap, AF.Exp, scale=-0.5, reads=[rr], writes=[rr])

        EPS_AP = tab[:, 768:769]
        ONE_AP = tab[:, 769:770]

        wmod_pref = {}
        for l in range(n_layers):
            src_d = xin_d if l == 0 else x1_d
            need_ctx = (l < DEPTH - 1)
            q_blocks = BLOCKS if need_ctx else BLOCKS[1:]
            q_tiles = list(range(NT)) if need_ctx else list(range(2, NT))

            ph01 = contextlib.ExitStack()
            xin = [sbt(ph01, "xin%d" % i, [128, D], F32) for i in range(4)]
            r_xin = [Res('xin%d' % i) for i in range(4)]
            crep = sbt(ph01, "crep", [128, 2, 8, 128], F32)
            bgt = sbt(ph01, "bgt", [128, D], F32)
            modrow = sbt(ph01, "modrow", [2, 2 * D], F32); r_modrow = Res('modrow')
            r_bgt = Res('bgt')
            xn = [sbt(ph01, "xn%d" % i, [128, D], BF16) for i in range(2)]
            r_xn = [Res('xn%d' % i) for i in range(2)]
            for j in range(2):
                for k in range(8):
                    op('dve', 'tensor_copy', crep[:, j, k, :], colbc(cT[:, k, j:j + 1], 128), reads=[r_c], writes=[r_c])
            dma(pp[:], pp_d.ap()[l], writes=[r_pp])
            dma(bgt[:], bass.AP(bgate_d, l * D, [[0, 128], [1, D]]), writes=[r_bgt])
            dma(lg[:], bass.AP(dec_d, l * 8, [[0, 128], [1, 8]]), writes=[r_lg])
            op('act', 'activation', lgt[:, 0, :], lg[:], AF.Exp, scale=-1.0, reads=[r_lg], writes=[r_lg])
            op('dve', 'tensor_scalar', lgt[:, 1, :], lgt[:, 0, :], 1.0, None, ALU.add, reads=[r_lg], writes=[r_lg])
            op('act', 'activation', lgt[:, 2, :], lgt[:, 1, :], AF.Ln, reads=[r_lg], writes=[r_lg])
            op('dve', 'tensor_scalar', lgt[:, 3, :], lgt[:, 1, :], -1.0, 1e-30, ALU.add, ALU.max, reads=[r_lg], writes=[r_lg])
            op('dve', 'reciprocal', lgt[:, 3, :], lgt[:, 3, :], reads=[r_lg], writes=[r_lg])
            op('dve', 'tensor_tensor', lgt[:, 3, :], lgt[:, 3, :], lgt[:, 0, :], ALU.mult, reads=[r_lg], writes=[r_lg])
            op('dve', 'scalar_tensor_tensor', lg[:], lgt[:, 2, :], -1.0, lgt[:, 3, :], ALU.mult, ALU.mult,
               reads=[r_lg], writes=[r_lg])

            mod_ps, r_modps = psA[:, 0:512], r_A[0]
            mod_v = mod_ps[:, 0:32].rearrange("p (c j) -> p c j", j=2)

            def p0_chunk(cb):
                if cb < len(wmod_pref.get(l, [])):
                    w32, rw = wmod_pref[l][cb]
                else:
                    w32, rw = load_w32(wmod_d, l * 1024 * 3072 + cb * 256, 3072, 256)
                if cb < 8:
                    ps, rp = nextS()
                    for k in range(8):
                        mm(ps[0:2, 0:256], cT[:, k, :], w32[:, k, :], k == 0, k == 7, [rw, r_c], [rp])
                    op('dve', 'tensor_copy', modrow[0:2, cb * 256:(cb + 1) * 256], ps[0:2, 0:256], reads=[rp], writes=[r_modrow])
                    if cb == 7:
                        for ci in range(16):
                            tp(mod_v[:, ci, :], modrow[0:2, ci * 128:(ci + 1) * 128], identf[0:2, 0:2], [r_modrow, r_ident], [r_modps])
                else:
                    n0 = (cb - 8) * 256
                    for j in (range(2) if need_ctx else range(1)):
                        ps, rp = nextS()
                        for k in range(8):
                            mm(ps[:, 0:256], crep[:, j, k, :], w32[:, k, :], k == 0, k == 7, [rw, r_c], [rp])
                        op('dve', 'tensor_tensor', gate[:, j, n0:n0 + 256], ps[:, 0:256], bgt[:, n0:n0 + 256], ALU.add,
                           reads=[rp, r_bgt], writes=[r_gate])

            def xsrc1(i):
                if i >= RES0:
                    return xres[:, i - RES0, :], r_xres[i - RES0], (l == 0)
                return xin[i % 4][:], r_xin[i % 4], True

            def p1_stats(i):
                xa, rx, need = xsrc1(i)
                if need:
                    dma(xa, src_d.ap()[i * 128:(i + 1) * 128, :], writes=[rx])
                s = i % 4
                op('dve', 'bn_stats', stat[:, s, 0:6], xa[:, 0:512], reads=[rx], writes=[r_stat[s]])
                op('dve', 'bn_stats', stat[:, s, 6:12], xa[:, 512:1024], reads=[rx], writes=[r_stat[s]])
                op('dve', 'bn_aggr', stat[:, s, 12:14], stat[:, s, 0:12], reads=[r_stat[s]], writes=[r_stat[s]])
                rstd_from_var(stat[:, s, 13:14], stat[:, s, 14:15], r_stat[s], stat[:, s, 15:16])

            def p1_norm(i):
                s = i % 4; xb = i % 2
                xa, rx, _ = xsrc1(i)
                op('dve', 'tensor_scalar', xn[xb][:], xa, stat[:, s, 12:13], stat[:, s, 14:15], ALU.subtract, ALU.mult,
                   reads=[rx, r_stat[s]], writes=[r_xn[xb]])
                psa_, rpa_ = nextG()
                psb_, rpb_ = nextS()
                va_ = psa_[:, :].bitcast(BF16).rearrange("p (k n) -> p k n", k=8)
                vb_ = psb_[:, :].bitcast(BF16).rearrange("p (k n) -> p k n", k=8)
                for k in range(8):
                    if k % 2 == 0:
                        tp(va_[:, k // 2, :], xn[xb][:, k * 128:(k + 1) * 128], ident[:], [r_xn[xb], r_ident], [rpa_])
                    else:
                        tp(vb_[:, k // 2, :], xn[xb][:, k * 128:(k + 1) * 128], ident[:], [r_xn[xb], r_ident], [rpb_])
                op('dve', 'tensor_copy', uT[:, 0:8:2, i * 128:(i + 1) * 128], va_[:, 0:4, :], reads=[rpa_], writes=[r_uTd[i]])
                op('act', 'copy', uT[:, 1:8:2, i * 128:(i + 1) * 128], vb_[:, 0:4, :], reads=[rpb_], writes=[r_uTa[i]])

            p1_steps = [lambda: p1_stats(0)]
            for i in range(NT):
                def step(i=i):
                    if i + 1 < NT:
                        p1_stats(i + 1)
                    p1_norm(i)
                p1_steps.append(step)
            p1_steps[0]()
            done = 1
            for cb in range(12):
                p0_chunk(cb)
                want = 1 + (cb + 1)
                while done < want:
                    p1_steps[done](); done += 1
            while done < len(p1_steps):
                p1_steps[done](); done += 1
            op('dve', 'tensor_tensor', modT[:], mod_v[:, 0:16, :], bc(pp[:, 0:16], 2, 2), ALU.add,
               reads=[r_modps, r_pp], writes=[r_mod])
            op('dve', 'tensor_scalar', modT[:, 8:16, :], modT[:, 8:16, :], 1.0, None, ALU.add, reads=[r_mod], writes=[r_mod])
            dump('modT%d' % l, modT[:], [r_mod])
            dump('gate%d' % l, gate[:], [r_gate])
            for k in range(8):
                for (j, lo, hi) in ((1, 0, CT), (0, CT, T)):
                    rr = [r for i in tiles(lo, hi - lo) for r in (r_uTd[i], r_uTa[i])]
                    if k % 2 == 0:
                        op('dve', 'tensor_scalar', uT[:, k, lo:hi], uT[:, k, lo:hi], modT[:, 8 + k, j:j + 1], modT[:, k, j:j + 1],
                           ALU.mult, ALU.add, reads=rr + [r_mod], writes=rr)
                    else:
                        op('act', 'activation', uT[:, k, lo:hi], uT[:, k, lo:hi], AF.Identity, bias=modT[:, k, j:j + 1], scale=modT[:, 8 + k, j:j + 1],
                           reads=rr + [r_mod], writes=rr)
            dump('uT%d' % l, uT[:], r_uT)
            WOFF = l * 1024 * WIN_EXT
            w_g2, rw_g2 = load_w(win_d, WOFF + 2 * 256, WIN_EXT, 256)
            w_rv, rw_rv = load_w(win_d, WOFF + 2496, WIN_EXT, 256)
            P.barrier()
            ph01.close()

            w, rw = w_g2, rw_g2
            for m in range(2):
                c = 4 + m

                def cons(ps, rp, t0, n, c=c):
                    op('act', 'activation', catT[:, c, t0:t0 + n], ps[:, 0:n], AF.Silu, reads=[rp],
                       writes=[r_cat[c][i] for i in tiles(t0, n)])
                proj_fm(w, rw, m * 128, 128, q_blocks, cons)
            order_g = [3, 0, 1]
            p2_units = [(gi, g, blk, bi == 0) for gi, g in enumerate(order_g) for bi, blk in enumerate(q_blocks)]
            p2_state = {'next': 0, 'loaded': 0}

            def load_wg(g, slot):
                b = wrot[0]; wrot[0] = 1 - b
                src = bass.AP(win_d, WOFF + g * 256, [[WIN_EXT, 128], [128 * WIN_EXT, 8], [1, 256]])
                dma(wst[b][:, :, :], src, writes=[r_wst[b]])
                op('dve', 'tensor_copy', wgate[slot][:], wst[b][:], reads=[r_wst[b]], writes=[r_wgate[slot]])

            def p2_tick(nmax=1):
                for _ in range(nmax):
                    if p2_state['next'] >= len(p2_units):
                        return
                    gi, g, (t0, n), first = p2_units[p2_state['next']]
                    p2_state['next'] += 1
                    slot = gi % 2
                    if first:
                        while p2_state['loaded'] <= min(gi + 1, len(order_g) - 1):
                            load_wg(order_g[p2_state['loaded']], p2_state['loaded'] % 2)
                            p2_state['loaded'] += 1
                    for m in range(2):
                        for k in range(8):
                            mm(psA[:, m * 512:m * 512 + n], wgate[slot][:, k, m * 128:(m + 1) * 128], uT[:, k, t0:t0 + n], k == 0, k == 7,
                               [r_wgate[slot]] + uTres(t0, n), [r_A[m]])
                    for m in range(2):
                        c = g * 2 + m
                        op('act', 'activation', catT[:, c, t0:t0 + n], psA[:, m * 512:m * 512 + n], AF.Silu, reads=[r_A[m]],
                           writes=[r_cat[c][i] for i in tiles(t0, n)])

            WOFF = l * 1024 * WIN_EXT
            if stage < 3:
                continue
            with contextlib.ExitStack() as ph:
                rq = sbt(ph, "rq", [128, T], BF16); rk = sbt(ph, "rk", [128, T], BF16)
                r_rq = [Res() for _ in range(NT)]; r_rk = [Res() for _ in range(NT)]
                QD = [sbt(ph, "QD%d" % h, [128, T], BF16) for h in range(2)]
                r_QD = [[Res() for _ in range(NT)] for h in range(2)]
                kf = sbt(ph, "kf", [128, NT, 128], BF16); kb = sbt(ph, "kb", [128, NT, 128], BF16)
                r_kfb = [Res() for _ in range(NT)]
                rv = sbt(ph, "rv", [128, NT, 256], BF16)
                r_rv = [Res() for _ in range(NT)]
                Scat = [sbt(ph, "Scat%d" % h, [128, NT, 64], BF16) for h in range(2)]
                r_Scat = [[Res() for _ in range(NT)] for h in range(2)]
                ST = sbt(ph, "ST", [128, 2, 128], F32); r_ST = [Res(), Res()]
                RC = sbt(ph, "RC", [128, T], BF16); RS = sbt(ph, "RS", [128, T], BF16); r_rt = Res()
                DcP = [sbt(ph, "Dc%d" % i, [128, 2, 128], F32) for i in range(2)]; DtP = [sbt(ph, "Dt%d" % i, [128, 2, 128], F32) for i in range(2)]; r_DcP = [Res(), Res()]
                QTP = [sbt(ph, "QT%d" % i, [128, 2, 2, 128], F32) for i in range(2)]; r_QTP = [Res(), Res()]
                KFP = [sbt(ph, "KFt%d" % i, [128, 2, 128], F32) for i in range(2)]; r_KFP = [Res(), Res()]
                gpP = [sbt(ph, "gp%d" % i, [128, 2], F32) for i in range(2)]; r_gpP = [Res(), Res()]
                rt1 = [sbt(ph, "rt1_%d" % i, [128, 512], F32) for i in range(2)]; r_rt1 = [Res(), Res()]
                qsb = [sbt(ph, "qsb%d" % i, [128, 512], BF16) for i in range(2)]; r_qsb = [Res(), Res()]
                rt2 = [sbt(ph, "rt2_%d" % i, [128, 512], F32) for i in range(2)]; r_rt2 = [Res(), Res()]
                PT2 = [sbt(ph, "PT2_%d" % i, [128, 2, 128], BF16) for i in range(3)]; r_PT2 = [Res() for _ in range(3)]
                gst = sbt(ph, "gst", [128, 4, 32], F32); r_gst = [Res() for _ in range(4)]
                on = [sbt(ph, "on%d" % i, [128, 128], BF16) for i in range(2)]; r_on = [Res(), Res()]
                ot = [sbt(ph, "ot%d" % i, [128, 128], BF16) for i in range(2)]; r_ot = [Res(), Res()]
                for bi_, (t0, n) in enumerate(BLOCKS):
                    for tb_, (dst_, src_) in enumerate(((RC, ropeRC_d), (RS, ropeRS_d))):
                        stg, rstg = (rt1[bi_ % 2], r_rt1[bi_ % 2]) if tb_ == 0 else (rt2[bi_ % 2], r_rt2[bi_ % 2])
                        dma(stg[:, 0:n], src_.ap()[:, t0:t0 + n], writes=[rstg])
                        op('dve', 'tensor_copy', dst_[:, t0:t0 + n], stg[:, 0:n], reads=[rstg], writes=[r_rt])
                def build_tables(hp):
                    Dt = DtP[hp]
                    Dc, r_Dc = DcP[hp], r_DcP[hp]; QT, r_QT = QTP[hp], r_QTP[hp]; KFt, r_KF = KFP[hp], r_KFP[hp]; gp, r_gp = gpP[hp], r_gpP[hp]
                    for hh in range(2):
                        h = hp * 2 + hh
                        op('act', 'activation', Dc[:, hh, :], tab[:, 0:128], AF.Exp, scale=lg[:, h:h + 1], reads=[r_tab, r_lg], writes=[r_Dc])
                        op('act', 'activation', Dt[:, hh, :], tab[:, 256:384], AF.Exp, scale=lg[:, 4 + h:5 + h], reads=[r_tab, r_lg], writes=[r_Dc])
                        op('dve', 'tensor_tensor', Dc[:, hh, :], Dc[:, hh, :], tab[:, 128:256], ALU.mult, reads=[r_Dc, r_tab], writes=[r_Dc])
                        op('dve', 'tensor_tensor', Dt[:, hh, :], Dt[:, hh, :], tab[:, 384:512], ALU.mult, reads=[r_Dc, r_tab], writes=[r_Dc])
                        op('dve', 'tensor_tensor', Dc[:, hh, :], Dc[:, hh, :], Dt[:, hh, :], ALU.add, reads=[r_Dc], writes=[r_Dc])
                        op('act', 'activation', QT[:, hh, 0, :], tab[:, 512:640], AF.Exp, scale=lg[:, h:h + 1], reads=[r_tab, r_lg], writes=[r_QT])
                        op('act', 'activation', QT[:, hh, 1, :], tab[:, 640:768], AF.Exp, scale=lg[:, 4 + h:5 + h], reads=[r_tab, r_lg], writes=[r_QT])
                        op('act', 'activation', KFt[:, 0, hh * 64:(hh + 1) * 64], colbc(tab[:, 127:128], 64),
                           AF.Exp, scale=lg[:, h:h + 1], reads=[r_tab, r_lg], writes=[r_KF])
                        op('act', 'activation', KFt[:, 1, hh * 64:(hh + 1) * 64], colbc(tab[:, 256:257], 64),
                           AF.Exp, scale=lg[:, 4 + h:5 + h], reads=[r_tab, r_lg], writes=[r_KF])
                        sl = slice(hh * 64, (hh + 1) * 64)
                        op('act', 'activation', gp[sl, 0:1], ONE_AP[sl, :], AF.Exp, scale=lg[sl, h:h + 1], reads=[r_tab, r_lg], writes=[r_gp])
                        op('act', 'activation', gp[sl, 1:2], ONE_AP[sl, :], AF.Exp, scale=lg[sl, 4 + h:5 + h], reads=[r_tab, r_lg], writes=[r_gp])
                    for _ in range(7):
                        op('dve', 'tensor_tensor', gp[:], gp[:], gp[:], ALU.mult, reads=[r_gp], writes=[r_gp])

                for hp_ in range(2):
                    build_tables(hp_)
                w, rw = w_rv, rw_rv
                for i in range(NT):
                    ps, rp = nextG()
                    for k in range(8):
                        mm(ps[:, 0:256], uT[:, k, i * 128:(i + 1) * 128], w[:, k, 0:256], k == 0, k == 7, [rw, r_uTd[i], r_uTa[i]], [rp])
                    op('act', 'copy', rv[:, i, :], ps[:, 0:256], reads=[rp], writes=[r_rv[i]])
                while p2_state['loaded'] < min(2, len(order_g)):
                    load_wg(order_g[p2_state['loaded']], p2_state['loaded'] % 2)
                    p2_state['loaded'] += 1
                prot = [0]
                for hp in range(2):
                    Dc, r_Dc = DcP[hp], r_DcP[hp]; QT, r_QT = QTP[hp], r_QTP[hp]; KFt, r_KF = KFP[hp], r_KFP[hp]; gp, r_gp = gpP[hp], r_gpP[hp]
                    wq_, rwq_ = load_w(win_d, WOFF + 1472 + hp * 512, WIN_EXT, 128)
                    wk_, rwk_ = load_w(win_d, WOFF + 1472 + hp * 512 + 256, WIN_EXT, 128)
                    pitems = [(wq_, rwq_, rq, r_rq, 1.0, blk) for blk in BLOCKS] + [(wk_, rwk_, rk, r_rk, 0.125, blk) for blk in BLOCKS]
                    pheld = {}

                    def pj_a(ii):
                        w, rw, dst, r_dst, sc, (t0, n) = pitems[ii]
                        psa, rpa = nextG()
                        for k in range(8):
                            mm(psa[:, 0:n], w[:, k, 0:128], uT[:, k, t0:t0 + n], k == 0, k == 7, [rw] + uTres(t0, n), [rpa])
                        pheld[ii] = (psa, rpa)

                    def pj_b(ii):
                        w, rw, dst, r_dst, sc, (t0, n) = pitems[ii]
                        psa, rpa = pheld.pop(ii)
                        b = ii % 2
                        op('dve', 'tensor_copy', qsb[b][:, 0:n], psa[:, 0:n], reads=[rpa], writes=[r_qsb[b]])
                        psb_, rpb = nextS()
                        mm(psb_[:, 0:n], permT[:], qsb[b][:, 0:n], True, True, [r_perm, r_qsb[b]], [rpb])
                        op('dve', 'scalar_tensor_tensor', rt1[b][:, 0:n], psa[:, 0:n], sc, RC[:, t0:t0 + n], ALU.mult, ALU.mult,
                           reads=[rpa, r_rt], writes=[r_rt1[b]])
                        op('dve', 'scalar_tensor_tensor', rt2[b][:, 0:n], psb_[:, 0:n], sc, RS[:, t0:t0 + n], ALU.mult, ALU.mult,
                           reads=[rpb, r_rt], writes=[r_rt2[b]])
                        op('pool', 'tensor_tensor', dst[:, t0:t0 + n], rt1[b][:, 0:n], rt2[b][:, 0:n], ALU.add,
                           reads=[r_rt1[b], r_rt2[b]], writes=[r_dst[i] for i in tiles(t0, n)])

                    pj_a(0)
                    for ii in range(len(pitems)):
                        if ii + 1 < len(pitems):
                            pj_a(ii + 1)
                        pj_b(ii)
                    dump('rq%d_%d' % (l, hp), rq[:], r_rq)
                    dump('rk%d_%d' % (l, hp), rk[:], r_rk)
                    for hh in range(2):
                        src_rows = slice(hh * 64, (hh + 1) * 64)
                        for (t0, n) in BLOCKS:
                            nb = n // 128
                            for d in range(2):
                                drows = slice(d * 64, (d + 1) * 64)
                                eng = 'pool' if d == hh else 'dve'
                                op(eng, 'tensor_tensor', QD[hh][drows, t0:t0 + n].rearrange("p (c i) -> p c i", i=128),
                                   rq[src_rows, t0:t0 + n].rearrange("p (c i) -> p c i", i=128), bc(QT[src_rows, hh, d, :], 1, nb), ALU.mult,
                                   reads=[r_rq[i] for i in tiles(t0, n)] + [r_QT], writes=[r_QD[hh][i] for i in tiles(t0, n)])
                    p2_tick(5)
                    for i in range(NT):
                        ps, rp = nextG()
                        psb = ps[:, :].bitcast(BF16)
                        tp(psb[:, 0:128], rk[:, i * 128:(i + 1) * 128], ident[:], [r_rk[i], r_ident], [rp])
                        op('dve', 'tensor_tensor', kf[:, i, :], psb[:, 0:128], KFt[:, 0, :], ALU.mult, reads=[rp, r_KF], writes=[r_kfb[i]])
                        op('dve', 'tensor_tensor', kb[:, i, :], psb[:, 0:128], KFt[:, 1, :], ALU.mult, reads=[rp, r_KF], writes=[r_kfb[i]])
                    order_f = list(range(NT))
                    order_b = list(range(2, NT)) + [0, 1]
                    dirs = ((0, order_f, kf), (1, order_b[::-1], kb))
                    for d in range(2):
                        op('dve', 'memset', ST[:, d, :], 0.0, writes=[r_ST[d]])
                    for pos in range(NT):
                        if pos % 8 == 3 and CUT != 1:
                            p2_tick(1)
                        for d, order, ksrc in dirs:
                            c = order[pos]
                            for hh in range(2):
                                drows = slice(d * 64, (d + 1) * 64)
                                op('act' if hh == d else 'dve', 'copy' if hh == d else 'tensor_copy', Scat[hh][drows, c, :], ST[hh * 64:(hh + 1) * 64, d, hh * 64:(hh + 1) * 64],
                                   reads=[r_ST[d]], writes=[r_Scat[hh][c]])
                            if pos == NT - 1:
                                continue
                            ps, rp = nextG()
                            mm(ps[:, 0:128], ksrc[:, c, :], rv[:, c, hp * 128:(hp + 1) * 128], True, True, [r_kfb[c], r_rv[c]], [rp])
                            op('dve', 'scalar_tensor_tensor', ST[:, d, :], ST[:, d, :], gp[:, d:d + 1], ps[:, 0:128], ALU.mult, ALU.add,
                               reads=[r_ST[d], r_gp, rp], writes=[r_ST[d]])
                    held = {}
                    psa_rot = [0]

                    psa_pool = [(psS[0], r_S[0]), (psS[1], r_S[1]), (psS[2], r_S[2]), (psA[:, 0:512], r_A[0])]

                    def st_a1(c):
                        cs = slice(c * 128, (c + 1) * 128)
                        got = []
                        for hh in range(2):
                            rows = slice(hh * 64, (hh + 1) * 64)
                            psa, rpa = psa_pool[psa_rot[0]]; psa_rot[0] = (psa_rot[0] + 1) % 4
                            mm(psa[:, 0:128], rk[rows, cs], rq[rows, cs], True, True, [r_rk[c], r_rq[c]], [rpa])
                            got.append((psa, rpa))
                        held[c] = got

                    def st_a2(c):
                        cs = slice(c * 128, (c + 1) * 128)
                        got = held[c]
                        pi = c % 3
                        for hh in range(2):
                            psa, rpa = got[hh]
                            op('dve', 'tensor_tensor', PT2[pi][:, hh, :], psa[:, 0:128], Dc[:, hh, :], ALU.mult,
                               reads=[rpa, r_Dc], writes=[r_PT2[pi]])
                        ops_, rpo = nextG()
                        for hh in range(2):
                            mm(ops_[:, hh:128:2], PT2[pi][:, hh, :], rv[:, c, (hp * 2 + hh) * 64:(hp * 2 + hh + 1) * 64], True, False,
                               [r_PT2[pi], r_rv[c]], [rpo])
                            mm(ops_[:, hh:128:2], QD[hh][:, cs], Scat[hh][:, c, :], False, True,
                               [r_QD[hh][c], r_Scat[hh][c]], [rpo])
                        held[c] = [ops_, rpo]

                    def st_b1(c):
                        ops_, rpo = held[c]
                        s = c % 4
                        op('dve', 'bn_stats', gst[:, s, 0:6], ops_[:, 0:128], reads=[rpo], writes=[r_gst[s]])
                        op('act', 'activation', gst[:, s, 18:20], gst[:, s, 2:6:3], AF.Ln, bias=EPS_AP, scale=1.0 / 64, reads=[r_gst[s], r_tab], writes=[r_gst[s]])
                        op('act', 'activation', gst[:, s, 16:18], gst[:, s, 18:20], AF.Exp, scale=-0.5, reads=[r_gst[s]], writes=[r_gst[s]])

                    def st_b2a(c):
                        ops_, rpo = held[c]
                        s = c % 4; b2 = c % 2
                        op('dve', 'scalar_tensor_tensor', gst[:, s, 20:22], gst[:, s, 1:5:3], -1.0, gst[:, s, 16:18], ALU.mult, ALU.mult,
                           reads=[r_gst[s]], writes=[r_gst[s]])
                        for hh in range(2):
                            op('act', 'activation', on[b2][:, hh * 64:(hh + 1) * 64], ops_[:, hh:128:2], AF.Identity,
                               bias=gst[:, s, 20 + hh:21 + hh], scale=gst[:, s, 16 + hh:17 + hh], reads=[rpo, r_gst[s]], writes=[r_on[b2]])

                    def st_b2b(c):
                        cs = slice(c * 128, (c + 1) * 128)
                        held.pop(c)
                        b2 = c % 2
                        pst, rpt = psA[:, 512:1024], r_A[1]
                        pstb = pst.bitcast(BF16)
                        tp(pstb[:, 0:128], on[b2][:], ident[:], [r_on[b2], r_ident], [rpt])
                        op('act', 'activation', ot[b2][:], pstb[:, 0:128], AF.Identity, bias=pp[:, 29 + hp:30 + hp], scale=pp[:, 27 + hp:28 + hp],
                           reads=[rpt, r_pp], writes=[r_ot[b2]])
                        op('pool', 'tensor_tensor', catT[:, 4 + hp, cs], ot[b2][:], catT[:, 4 + hp, cs], ALU.mult,
                           reads=[r_ot[b2], r_cat[4 + hp][c]], writes=[r_cat[4 + hp][c]])

                    stages_ = [st_a1, st_a2, st_b1, st_b2a, st_b2b]
                    nqt = len(q_tiles)
                    for t_ in range(nqt + len(stages_) - 1):
                        for si_, fn_ in enumerate(stages_):
                            ix_ = t_ - si_
                            if 0 <= ix_ < nqt:
                                fn_(q_tiles[ix_])
                p2_tick(999)
                wg, rwg = load_w(win_d, WOFF + 3008, WIN_EXT, 256)
                wa, rwa = load_w(win_d, WOFF + 2752, WIN_EXT, 256)
                dump('catret%d' % l, catT[:, 4:6, :], r_cat[4] + r_cat[5])
                P.barrier()

            phm = contextlib.ExitStack()
            RM = sbt(phm, "RM", [128, T], F32); r_RM = Res()
            wuqb = sbt(phm, "wuqb", [128, 2, 1024], BF16); r_wuq = Res()
            wukvb = sbt(phm, "wukvb", [128, 1024], BF16); r_wukv = Res()
            wuqf = wst[0][:].rearrange("p k n -> p (k n)").rearrange("p (k n) -> p k n", k=2)
            wukvf = wst[1][:].rearrange("p k n -> p (k n)")[:, 0:1024]
            dma(RM[64:128, :], ropeM_d.ap()[64:128, :], writes=[r_RM])
            dma(wuqf, bass.AP(wuq_d, l * 256 * 1024, [[1024, 128], [128 * 1024, 2], [1, 1024]]), writes=[r_wst[0]])
            dma(wukvf, wukv_d.ap()[l], writes=[r_wst[1]])
            for k in range(2):
                op('dve', 'tensor_scalar', wuqb[:, k, :], wuqf[:, k, :], pp[:, 24 + k:25 + k], None, ALU.mult, reads=[r_wst[0], r_pp], writes=[r_wuq])
            op('dve', 'tensor_scalar', wukvb[:], wukvf, pp[:, 26:27], None, ALU.mult, reads=[r_wst[1], r_pp], writes=[r_wukv])

            if stage < 4:
                continue
            with contextlib.ExitStack() as ph:
                U = sbt(ph, "U", [128, 2, UW], BF16); r_U = Res()
                diag = sbt(ph, "diag", [128, 62, 128], BF16); r_diag = Res()
                sig = [sbt(ph, "sig%d" % i, [128, 512], F32) for i in range(2)]; r_sig = [Res(), Res()]
                Y = [sbt(ph, "Y%d" % i, [128, 2, 512], F32) for i in range(2)]; r_Y = [Res(), Res()]
                Y2 = [sbt(ph, "Y2_%d" % i, [128, 2, 512], BF16) for i in range(2)]; r_Y2 = [Res(), Res()]
                Yb = [sbt(ph, "Yb_%d" % i, [128, 2, 512], BF16) for i in range(2)]; r_Yb = [Res(), Res()]
                cs_ = [sbt(ph, "cst0", [128, 3, 512], F32)] * 2; r_cs = [Res()] * 2
                Z = [sbt(ph, "Z%d" % i, [128, 2, 512], BF16) for i in range(2)]; r_Z = [Res(), Res()]
                pwb = sbt(ph, "pwb", [128, 2, 256], BF16); pwf = sbt(ph, "pwf", [128, 2, 256], F32); r_pw = Res()
                op('pool', 'memset', U[:], 0.0, writes=[r_U])
                diag_todo = [(ch, kk) for ch in range(2) for kk in range(31)]

                def diag_tick(nmax):
                    for _ in range(nmax):
                        if not diag_todo:
                            return
                        ch, kk = diag_todo.pop(0)
                        op('dve', 'tensor_scalar', diag[:, ch * 31 + kk, :], ident[:], pp[:, 39 + ch * 31 + kk:40 + ch * 31 + kk], None, ALU.mult,
                           reads=[r_ident, r_pp], writes=[r_diag])
                dma(pwf[:], bass.AP(pw_d, l * 256 * 256, [[256, 128], [128 * 256, 2], [1, 256]]), writes=[r_pw])
                op('pool', 'tensor_copy', pwb[:], pwf[:], reads=[r_pw], writes=[r_pw])
                srot_ = [0]
                for ch in range(2):
                    for (t0, n) in q_blocks:
                        psg, rpg = nextG()
                        for k in range(8):
                            mm(psg[:, 0:n], wg[:, k, ch * 128:(ch + 1) * 128], uT[:, k, t0:t0 + n], k == 0, k == 7, [rwg] + uTres(t0, n), [rpg])
                        psa, rpa = nextG()
                        for k in range(8):
                            mm(psa[:, 0:n], wa[:, k, ch * 128:(ch + 1) * 128], uT[:, k, t0:t0 + n], k == 0, k == 7, [rwa] + uTres(t0, n), [rpa])
                        b = srot_[0]; srot_[0] = 1 - b
                        op('act', 'activation', sig[b][:, 0:n], psg[:, 0:n], AF.Sigmoid, reads=[rpg], writes=[r_sig[b]])
                        u0 = (UOFF_C + t0) if t0 < CT else (UOFF_L + t0 - CT)
                        op('dve', 'tensor_tensor', U[:, ch, u0:u0 + n], psa[:, 0:n], sig[b][:, 0:n], ALU.mult, reads=[rpa, r_sig[b]], writes=[r_U])
                        diag_tick(8)
                diag_tick(99)

                def conv_dw(bi):
                    t0, n = q_blocks[bi]; b = bi % 2
                    u0 = (UOFF_C + t0) if t0 < CT else (UOFF_L + t0 - CT)
                    for ch in range(2):
                        ps, rp = nextS()
                        for kk in range(31):
                            mm(ps[:, 0:n], diag[:, ch * 31 + kk, :], U[:, ch, u0 + kk - 15:u0 + kk - 15 + n], kk == 0, kk == 30, [r_diag, r_U], [rp])
                        op('dve', 'tensor_scalar', Y[b][:, ch, 0:n], ps[:, 0:n], pp[:, 31 + ch:32 + ch], None, ALU.add, reads=[rp, r_pp], writes=[r_Y[b]])
                        op('pool', 'tensor_tensor', Y2[b][:, ch, 0:n], Y[b][:, ch, 0:n], Y[b][:, ch, 0:n], ALU.mult, reads=[r_Y[b]], writes=[r_Y2[b]])
                        op('act', 'copy', Yb[b][:, ch, 0:n], Y[b][:, ch, 0:n], reads=[r_Y[b]], writes=[r_Yb[b]])

                def conv_post(bi):
                    t0, n = q_blocks[bi]; b = bi % 2
                    ps1, rp1 = nextG()
                    for ch in range(2):
                        mm(ps1[:, 0:n], ones_bf[:], Yb[b][:, ch, 0:n], ch == 0, ch == 1, [r_ones, r_Yb[b]], [rp1])
                    ps2, rp2 = nextG()
                    for ch in range(2):
                        mm(ps2[:, 0:n], ones_bf[:], Y2[b][:, ch, 0:n], ch == 0, ch == 1, [r_ones, r_Y2[b]], [rp2])
                    mean = cs_[b][:, 0, 0:n]; var = cs_[b][:, 1, 0:n]; rstd = cs_[b][:, 1, 0:n]; tmp = cs_[b][:, 2, 0:n]
                    op('dve', 'tensor_scalar', mean, ps1[:, 0:n], 1.0 / 256, None, ALU.mult, reads=[rp1], writes=[r_cs[b]])
                    op('dve', 'tensor_tensor', var, mean, mean, ALU.mult, reads=[r_cs[b]], writes=[r_cs[b]])
                    op('dve', 'scalar_tensor_tensor', var, ps2[:, 0:n], 1.0 / 256, var, ALU.mult, ALU.subtract, reads=[rp2, r_cs[b]], writes=[r_cs[b]])
                    op('act', 'activation', tmp, var, AF.Ln, bias=EPS_AP, reads=[r_cs[b], r_tab], writes=[r_cs[b]])
                    op('act', 'activation', rstd, tmp, AF.Exp, scale=-0.5, reads=[r_cs[b]], writes=[r_cs[b]])
                    for ch in range(2):
                        op('dve', 'tensor_tensor', Y[b][:, ch, 0:n], Y[b][:, ch, 0:n], mean, ALU.subtract, reads=[r_Y[b], r_cs[b]], writes=[r_Y[b]])
                        op('dve', 'tensor_tensor', Y[b][:, ch, 0:n], Y[b][:, ch, 0:n], rstd, ALU.mult, reads=[r_Y[b], r_cs[b]], writes=[r_Y[b]])
                    for ch in range(2):
                        op('act', 'activation', Z[b][:, ch, 0:n], Y[b][:, ch, 0:n], AF.Silu, bias=pp[:, 35 + ch:36 + ch], scale=pp[:, 33 + ch:34 + ch],
                           reads=[r_Y[b], r_pp], writes=[r_Z[b]])

                def conv_post_b(bi):
                    t0, n = q_blocks[bi]; b = bi % 2
                    for oc in range(2):
                        ps, rp = nextG()
                        for ic in range(2):
                            mm(ps[:, 0:n], pwb[:, ic, oc * 128:(oc + 1) * 128], Z[b][:, ic, 0:n], ic == 0, ic == 1, [r_pw, r_Z[b]], [rp])
                        rc = [r_cat[6 + oc][i] for i in tiles(t0, n)]
                        op('dve', 'scalar_tensor_tensor', catT[:, 6 + oc, t0:t0 + n], ps[:, 0:n], pp[:, 37 + oc:38 + oc], catT[:, 6 + oc, t0:t0 + n],
                           ALU.add, ALU.mult, reads=[rp, r_pp] + rc, writes=rc)

                conv_dw(0)
                for bi in range(len(q_blocks)):
                    conv_post(bi)
                    if bi + 1 < len(q_blocks):
                        conv_dw(bi + 1)
                    conv_post_b(bi)
                dump('catconv%d' % l, catT[:, 6:8, :], r_cat[6] + r_cat[7])
                w1, rw1 = load_w(win_d, WOFF + 1024, WIN_EXT, 256)
                w2, rw2 = load_w(win_d, WOFF + 1280, WIN_EXT, 192)
                P.barrier()

            if stage < 4.2:
                continue
            with contextlib.ExitStack() as ph:
                pqn = sbt(ph, "pqn", [128, 2, T], BF16); r_pqn = [Res() for _ in range(NT)]
                pkvn = sbt(ph, "pkvn", [128, 1, T], BF16); r_pkvn = [Res() for _ in range(NT)]
                rt1 = [sbt(ph, "mrt1_0", [128, 512], F32)] * 2; r_rt1 = [Res()] * 2
                rt2 = [sbt(ph, "mrt2_0", [128, 512], F32)] * 2; r_rt2 = [Res()] * 2
                kT = [sbt(ph, "kT%d" % i, [128, T], BF16) for i in range(2)]; r_kT = [Res(), Res()]
                for i in range(2):
                    op('pool', 'memset', kT[i][:], 0.0, writes=[r_kT[i]])
                ph_a = contextlib.ExitStack()
                sq = [sbt(ph_a, "sq%d" % i, [128, 2, 512], BF16) for i in range(2)]; r_sq = [Res(), Res()]
                pqf = [sbt(ph_a, "pqf%d" % i, [128, 2, 512], F32) for i in range(2)]; r_pqf = [Res(), Res()]
                rr_ = [sbt(ph_a, "rr%d" % i, [128, 2, 512], F32) for i in range(2)]; r_rr = [Res(), Res()]
                items = [(w1, rw1, 2, pqn, r_pqn, blk, 1.0 / 256) for blk in q_blocks] + \
                        [(w2, rw2, 1, pkvn, r_pkvn, blk, 1.0 / 128) for blk in BLOCKS]

                def rms_x(ii):
                    w, rw, nch, dst, r_dst, (t0, n), inv_n = items[ii]; b = ii % 2
                    for cch in range(nch):
                        ps, rp = nextG()
                        for k in range(8):
                            mm(ps[:, 0:n], w[:, k, cch * 128:(cch + 1) * 128], uT[:, k, t0:t0 + n], k == 0, k == 7, [rw] + uTres(t0, n), [rp])
                        op('dve', 'tensor_copy', pqf[b][:, cch, 0:n], ps[:, 0:n], reads=[rp], writes=[r_pqf[b]])
                        op('dve', 'tensor_tensor', sq[b][:, cch, 0:n], pqf[b][:, cch, 0:n], pqf[b][:, cch, 0:n], ALU.mult, reads=[r_pqf[b]], writes=[r_sq[b]])

                def rms_y(ii):
                    w, rw, nch, dst, r_dst, (t0, n), inv_n = items[ii]; b = ii % 2
                    pss, rps = nextS()
                    for cch in range(nch):
                        mm(pss[:, 0:n], ones_bf[:], sq[b][:, cch, 0:n], cch == 0, cch == nch - 1, [r_ones, r_sq[b]], [rps])
                    op('act', 'activation', rr_[b][:, 0, 0:n], pss[:, 0:n], AF.Ln, bias=EPS_AP, scale=inv_n, reads=[rps, r_tab], writes=[r_rr[b]])
                    op('act', 'activation', rr_[b][:, 1, 0:n], rr_[b][:, 0, 0:n], AF.Exp, scale=-0.5, reads=[r_rr[b]], writes=[r_rr[b]])
                    for cch in range(nch):
                        op('dve', 'tensor_tensor', dst[:, cch, t0:t0 + n], pqf[b][:, cch, 0:n], rr_[b][:, 1, 0:n], ALU.mult,
                           reads=[r_pqf[b], r_rr[b]], writes=[r_dst[i] for i in tiles(t0, n)])

                def rope_key(j):
                    t0, n = BLOCKS[j]
                    ps, rp = nextG()
                    for k in range(8):
                        mm(ps[:, 0:n], w2[:, k, 64:192], uT[:, k, t0:t0 + n], k == 0, k == 7, [rw2] + uTres(t0, n), [rp])
                    op('dve', 'tensor_tensor', rt1[0][64:96, 0:n], ps[64:96, 0:n], RM[64:96, t0:t0 + n], ALU.mult, reads=[rp, r_RM], writes=[r_rt1[0]])
                    op('dve', 'tensor_tensor', rt2[0][64:96, 0:n], ps[96:128, 0:n], RM[96:128, t0:t0 + n], ALU.mult, reads=[rp, r_RM], writes=[r_rt2[0]])
                    op('pool', 'tensor_tensor', kT[0][64:96, t0:t0 + n], rt1[0][64:96, 0:n], rt2[0][64:96, 0:n], ALU.add,
                       reads=[r_rt1[0], r_rt2[0]], writes=[r_kT[0]])
                    op('dve', 'tensor_tensor', kT[1][64:96, t0:t0 + n], rt1[0][64:96, 0:n], rt2[0][64:96, 0:n], ALU.add,
                       reads=[r_rt1[0], r_rt2[0]], writes=[r_kT[1]])

                rms_x(0)
                for ii in range(len(items)):
                    if ii + 1 < len(items):
                        rms_x(ii + 1)
                    rms_y(ii)
                    if ii % 2 == 1 and ii // 2 < len(BLOCKS):
                        rope_key(ii // 2)
                for j in range(len(items) // 2, len(BLOCKS)):
                    rope_key(j)
                rrot = [0]
                dump('pqn%d' % l, pqn[:], r_pqn)
                dump('pkvn%d' % l, pkvn[:], r_pkvn)

                P.barrier()
                ph_a.close()
                if stage < 4.5:
                    continue
                qT = [sbt(ph, "qT%d" % i, [128, T], BF16) for i in range(2)]; r_qT = [Res(), Res()]
                va = [sbt(ph, "va%d" % i, [128, NT, 128], BF16) for i in range(2)]; r_va = [Res(), Res()]
                pT = [sbt(ph, "pT%d" % i, [128, 512], BF16) for i in range(4)]; r_pT = [Res() for _ in range(4)]
                rden = [sbt(ph, "rden%d" % i, [128, 512], F32) for i in range(2)]; r_rden = [Res(), Res()]
                ot2 = [sbt(ph, "ot2_%d" % i, [128, 512], F32) for i in range(2)]; r_ot2 = [Res(), Res()]
                for i in range(2):
                    op('pool', 'memset', qT[i][:], 0.0, writes=[r_qT[i]])
                    op('pool', 'memset', va[i][:], 1.0, writes=[r_va[i]])

                NB = len(BLOCKS)
                r_qn = [[Res() for _ in range(NB)] for _ in range(2)]
                r_qr = [[Res() for _ in range(NB)] for _ in range(2)]
                r_kn = [[Res() for _ in range(NB)] for _ in range(2)]
                r_vb = [[Res() for _ in range(NB)] for _ in range(2)]

                def piece(h, j):
                    hb = h % 2
                    t0, n = BLOCKS[j]
                    if (t0, n) in q_blocks:
                        ps, rp = nextG()
                        for k in range(2):
                            mm(ps[:, 0:n], wuqb[:, k, h * 128:(h + 1) * 128], pqn[:, k, t0:t0 + n], k == 0, k == 1,
                               [r_wuq] + [r_pqn[i] for i in tiles(t0, n)], [rp])
                        op('dve', 'tensor_copy', qT[hb][0:64, t0:t0 + n], ps[0:64, 0:n], reads=[rp, r_qT[hb]], writes=[r_qn[hb][j]])
                        op('dve', 'tensor_tensor', rt1[0][64:96, 0:n], ps[64:96, 0:n], RM[64:96, t0:t0 + n], ALU.mult, reads=[rp, r_RM], writes=[r_rt1[0]])
                        op('dve', 'tensor_tensor', rt2[0][64:96, 0:n], ps[96:128, 0:n], RM[96:128, t0:t0 + n], ALU.mult, reads=[rp, r_RM], writes=[r_rt2[0]])
                        op('pool', 'tensor_tensor', qT[hb][64:96, t0:t0 + n], rt1[0][64:96, 0:n], rt2[0][64:96, 0:n], ALU.add,
                           reads=[r_rt1[0], r_rt2[0], r_qT[hb]], writes=[r_qr[hb][j]])
                    ps, rp = nextG()
                    mm(ps[0:64, 0:n], wukvb[:, h * 64:(h + 1) * 64], pkvn[:, 0, t0:t0 + n], True, True,
                       [r_wukv] + [r_pkvn[i] for i in tiles(t0, n)], [rp])
                    op('dve', 'tensor_copy', kT[hb][0:64, t0:t0 + n], ps[0:64, 0:n], reads=[rp, r_kT[hb]], writes=[r_kn[hb][j]])
                    ps, rp = nextG()
                    i0 = t0 // 128; nb = n // 128
                    for ii in range(nb):
                        i = i0 + ii
                        mm(ps[:, ii * 64:(ii + 1) * 64], pkvn[:, 0, i * 128:(i + 1) * 128], wukvb[:, 512 + h * 64:512 + (h + 1) * 64], True, True,
                           [r_wukv, r_pkvn[i]], [rp])
                    op('dve', 'tensor_copy', va[hb][:, i0:i0 + nb, 0:64], ps[:, 0:nb * 64].rearrange("p (i e) -> p i e", e=64),
                       reads=[rp, r_va[hb]], writes=[r_vb[hb][j]])

                prot2 = [0]; orot = [0]
                SCALE = 96 ** -0.5

                def attend(h, inserts):
                    hb = h % 2
                    for bi, (t0, n) in enumerate(q_blocks):
                        jq = BLOCKS.index((t0, n))
                        kts = [0, 1] if t0 < CT else list(range(NT))
                        oi = orot[0]; orot[0] = 1 - oi
                        po = psA[:, oi * 512:oi * 512 + 512]; rpo = r_A[oi]
                        sps = {}

                        def issue_s(kt):
                            ps, rp = nextS()
                            jk = 0 if kt < 2 else 1 + (kt - 2) // 4
                            mm(ps[:, 0:n], kT[hb][:, kt * 128:(kt + 1) * 128], qT[hb][:, t0:t0 + n], True, True,
                               [r_kT[hb], r_kn[hb][jk], r_qn[hb][jq], r_qr[hb][jq]], [rp])
                            sps[kt] = (ps, rp)
                        issue_s(kts[0])
                        if len(kts) > 1:
                            issue_s(kts[1])
                        ins_at = min(3, len(kts) - 1)
                        for idx, kt in enumerate(kts):
                            if idx + 2 < len(kts):
                                issue_s(kts[idx + 2])
                            if idx == ins_at:
                                for f in inserts[bi]:
                                    f()
                            ps, rp = sps.pop(kt)
                            pi = prot2[0]; prot2[0] = (pi + 1) % 4
                            jk = 0 if kt < 2 else 1 + (kt - 2) // 4
                            op('act', 'activation', pT[pi][:, 0:n], ps[:, 0:n], AF.Exp, scale=SCALE, reads=[rp], writes=[r_pT[pi]])
                            mm(po[:, 0:n], va[hb][:, kt, :], pT[pi][:, 0:n], idx == 0, idx == len(kts) - 1, [r_va[hb], r_vb[hb][jk], r_pT[pi]], [rpo])
                        rb = oi
                        c = h // 2; rows = slice((h % 2) * 64, (h % 2) * 64 + 64)
                        op('dve', 'reciprocal', rden[rb][rows, 0:n], po[64:128, 0:n], reads=[rpo], writes=[r_rden[rb]])
                        op('dve', 'tensor_tensor', ot2[rb][rows, 0:n], po[0:64, 0:n], rden[rb][rows, 0:n], ALU.mult, reads=[rpo, r_rden[rb]], writes=[r_ot2[rb]])
                        rc = [r_cat[c][i] for i in tiles(t0, n)]
                        op('pool', 'tensor_tensor', catT[rows, c, t0:t0 + n], ot2[rb][rows, 0:n], catT[rows, c, t0:t0 + n], ALU.mult,
                           reads=[r_ot2[rb]] + rc, writes=rc)

                for j in range(NB):
                    piece(0, j)
                uflat = uT[:].rearrange("p k t -> p (k t)")
                woA = [uflat[:, j * 8192:(j + 1) * 8192].rearrange("p (k n) -> p k n", k=8) for j in range(2)]
                r_wo = Res('wo')
                nq = len(q_blocks)
                fold_pending = []
                for h in (range(8) if stage >= 5 else []):
                    inserts = [[] for _ in range(nq)]
                    if h + 1 < 8:
                        for j in range(NB):
                            slot = min(max(j - (NB - nq), 0), nq - 1)
                            inserts[slot].append(lambda h=h, j=j: piece(h + 1, j))
                    for fi_, f_ in enumerate(fold_pending):
                        inserts[fi_ % nq].append(f_)
                    fold_pending = []
                    attend(h, inserts)
                    if 1 <= h <= 4:
                        nh = h - 1
                        b = wrot[0]; wrot[0] = 1 - b
                        src = bass.AP(wout_d, l * D * D + nh * 256, [[D, 128], [128 * D, 8], [1, 256]])
                        dma(wst[b][:, :, :], src, writes=[r_wst[b]])
                        for j in (range(2) if need_ctx else range(1)):
                            for kk in range(4):
                                def fold_(b=b, j=j, kk=kk, nh=nh):
                                    op('dve', 'tensor_tensor', woA[j][:, 2 * kk:2 * kk + 2, nh * 256:(nh + 1) * 256], wst[b][:, 2 * kk:2 * kk + 2, :],
                                       bc(gate[:, j, nh * 256:(nh + 1) * 256], 1, 2), ALU.mult, reads=[r_wst[b], r_gate], writes=[r_wo])
                                fold_pending.append(fold_)
                dump('qT%d' % l, qT[0][:], [r_qT[0]])
                dump('kT%d' % l, kT[0][:], [r_kT[0]])
                dump('va%d' % l, va[0][:], [r_va[0]])
                dump('catmla%d' % l, catT[:, 0:4, :], r_cat[0] + r_cat[1] + r_cat[2] + r_cat[3])
                P.barrier()

            phm.close()

            if stage < 6:
                continue
            with contextlib.ExitStack() as ph:
                xin = [sbt(ph, "xin%d" % i, [128, D], F32) for i in range(3)]
                r_xin = [Res('xin%d' % i) for i in range(3)]
                lnp = sbt(ph, "lnp", [128, 2, D], F32); r_lnp = Res('lnp')
                dma(lnp[:, 0, :], bass.AP(lng_d, l * D, [[0, 128], [1, D]]), writes=[r_lnp])
                dma(lnp[:, 1, :], bass.AP(lnb_d, l * D, [[0, 128], [1, D]]), writes=[r_lnp])
                z = [sbt(ph, "z%d" % i, [128, D], F32)[:] for i in range(3)]; r_z = [Res(), Res(), Res()]
                z2 = [sbt(ph, "zn%d" % i, [128, D], F32)[:] for i in range(3)]; r_z2 = [Res(), Res(), Res()]
                pairs = [((psA[:, 0:512], r_A[0]), (psA[:, 512:1024], r_A[1])), ((psS[0], r_S[0]), (psS[1], r_S[1]))]
                if l == 0:
                    r_x1 = [Res('x1_%d' % i) for i in range(NT)]

                def p6_load(i):
                    if i >= RES0:
                        return
                    b = i % 3
                    dma(xin[b][:], src_d.ap()[i * 128:(i + 1) * 128, :], writes=[r_xin[b]])

                junk = sbt(ph, "junk", [128, D], BF16)[:]; r_junk = Res('junk')

                def p6_m(i):
                    pr = i % 2
                    j = 1 if i < 2 else 0
                    for nh in range(2):
                        ps, rp = pairs[pr][nh]
                        for k in range(8):
                            mm(ps[:, 0:512], catT[:, k, i * 128:(i + 1) * 128], woA[j][:, k, nh * 512:(nh + 1) * 512], k == 0, k == 7,
                               [r_cat[k][i], r_wo], [rp])
                    b = i % 3; zb = i % 3; s = i % 4
                    if i >= RES0:
                        xa, rx = xres[:, i - RES0, :], r_xres[i - RES0]
                    else:
                        xa, rx = xin[b][:], r_xin[b]
                    for nh in range(2):
                        ps, rp = pairs[pr][nh]
                        op('dve', 'scalar_tensor_tensor', z[zb][:, nh * 512:(nh + 1) * 512], xa[:, nh * 512:(nh + 1) * 512], ALPHA, ps[:, 0:512],
                           ALU.mult, ALU.add, accum_out=stat[:, s, nh:nh + 1], reads=[rx, rp], writes=[r_z[zb], r_stat[s]])
                    op('act', 'activation', junk, z[zb], AF.Square, accum_out=stat[:, s, 2:3], reads=[r_z[zb]], writes=[r_junk, r_stat[s]])

                def p6_n1(i):
                    s = i % 4
                    op('dve', 'tensor_tensor', stat[:, s, 3:4], stat[:, s, 0:1], stat[:, s, 1:2], ALU.add, reads=[r_stat[s]], writes=[r_stat[s]])
                    op('dve', 'tensor_scalar', stat[:, s, 3:4], stat[:, s, 3:4], 1.0 / D, None, ALU.mult, reads=[r_stat[s]], writes=[r_stat[s]])
                    op('dve', 'tensor_tensor', stat[:, s, 4:5], stat[:, s, 3:4], stat[:, s, 3:4], ALU.mult, reads=[r_stat[s]], writes=[r_stat[s]])
                    op('dve', 'scalar_tensor_tensor', stat[:, s, 5:6], stat[:, s, 2:3], 1.0 / D, stat[:, s, 4:5], ALU.mult, ALU.subtract,
                       reads=[r_stat[s]], writes=[r_stat[s]])
                    rstd_from_var(stat[:, s, 5:6], stat[:, s, 7:8], r_stat[s], stat[:, s, 6:7])

                def p6_n2(i):
                    s = i % 4
                    op('dve', 'scalar_tensor_tensor', stat[:, s, 8:9], stat[:, s, 3:4], -1.0, stat[:, s, 7:8], ALU.mult, ALU.mult,
                       reads=[r_stat[s]], writes=[r_stat[s]])
                    op('act', 'activation', z2[i % 3], z[i % 3], AF.Identity, bias=stat[:, s, 8:9], scale=stat[:, s, 7:8],
                       reads=[r_z[i % 3], r_stat[s]], writes=[r_z2[i % 3]])

                def p6_f(i):
                    zb = i % 3
                    op('dve', 'tensor_tensor', z2[zb], z2[zb], lnp[:, 0, :], ALU.mult, reads=[r_z2[zb], r_lnp], writes=[r_z2[zb]])
                    if i >= RES0 and l < n_layers - 1:
                        op('pool', 'tensor_tensor', xres[:, i - RES0, :], z2[zb], lnp[:, 1, :], ALU.add,
                           reads=[r_z2[zb], r_lnp], writes=[r_xres[i - RES0]])
                        return
                    op('pool', 'tensor_tensor', z2[zb], z2[zb], lnp[:, 1, :], ALU.add, reads=[r_z2[zb], r_lnp], writes=[r_z2[zb]])
                    if l == n_layers - 1 and i >= 2:
                        dma(out_d.ap()[(i - 2) * 128:(i - 1) * 128, :], z2[zb], reads=[r_z2[zb]], is_output=True)
                    else:
                        dma(x1_d.ap()[i * 128:(i + 1) * 128, :], z2[zb], reads=[r_z2[zb]], writes=[r_x1[i]], is_output=True)

                qt = q_tiles
                nq6 = len(qt)
                p6_stages = [(0, p6_load), (2, p6_m), (3, p6_n1), (4, p6_n2), (5, p6_f)]
                for t_ in range(nq6 + 5):
                    for lag_, fn_ in p6_stages:
                        ix_ = t_ - lag_
                        if 0 <= ix_ < nq6:
                            fn_(qt[ix_])
                if l + 1 < n_layers:
                    wmod_pref[l + 1] = [load_w32(wmod_d, (l + 1) * 1024 * 3072 + cb * 256, 3072, 256) for cb in range(2)]
                P.barrier()
        P.emit()
    return nc, dbg_outs


def _consts():
    tab = np.zeros((128, 6 * 128 + 2), np.float32)
    j = np.arange(128)[:, None].astype(np.float64); i = np.arange(128)[None, :].astype(np.float64)
    tab[:, 0:128] = np.maximum(i - j, 0)
    tab[:, 128:256] = (i >= j)
    tab[:, 256:384] = np.maximum(j - i, 0)
    tab[:, 384:512] = (j > i)
    tab[:, 512:640] = np.broadcast_to(i + 1, (128, 128))
    tab[:, 640:768] = np.broadcast_to(128 - i, (128, 128))
    tab[:, 768] = EPS
    tab[:, 769] = 1.0
    pos = np.arange(L)
    row = (pos // 64).astype(np.float64); col = (pos % 64).astype(np.float64)

    def cs(d2, p):
        inv = 10000.0 ** (-np.arange(0, d2, 2) / d2)
        ang = p[None, :] * inv[:, None]
        return np.cos(ang), np.sin(ang)
    cr, sr = cs(16, row); cc, sc = cs(16, col)
    C = np.concatenate([cr, cr, cc, cc], 0); S = np.concatenate([-sr, sr, -sc, sc], 0)
    ropeM = np.zeros((128, T), np.float32)
    ropeM[64:96, :CT] = 1.0
    ropeM[64:96, CT:] = C
    ropeM[96:128, CT:] = S
    cr, sr = cs(32, row); cc, sc = cs(32, col)
    C = np.concatenate([cr, cr, cc, cc], 0); S = np.concatenate([-sr, sr, -sc, sc], 0)
    ropeRC = np.ones((128, T), np.float32); ropeRS = np.zeros((128, T), np.float32)
    ropeRC[:, CT:] = np.concatenate([C, C], 0); ropeRS[:, CT:] = np.concatenate([S, S], 0)
    return tab, ropeM, ropeRC, ropeRS


def _prep_shared(inp):
    f = lambda a: np.ascontiguousarray(a, dtype=np.float32)
    w_in = inp['w_in']
    o = dict(pq=0, pkv=256, pkr=384, pg_mla=416, prq=928, prk=1184, prv=1440, pg_ret=1696, pglu=1952, pg_conv=2464)
    sw32 = np.concatenate([np.arange(8, 16), np.arange(0, 8), np.arange(24, 32), np.arange(16, 24)])
    sw64 = np.concatenate([np.arange(16, 32), np.arange(0, 16), np.arange(48, 64), np.arange(32, 48)])
    cols = []
    cols += list(range(o['pg_mla'], o['pg_mla'] + 512))
    cols += list(range(o['pg_ret'], o['pg_ret'] + 256))
    cols += list(range(o['pg_conv'], o['pg_conv'] + 256))
    cols += list(range(o['pq'], o['pq'] + 256))
    cols += list(range(o['pkv'], o['pkv'] + 128))
    cols += list(range(o['pkr'], o['pkr'] + 32))
    cols += list(o['pkr'] + sw32)
    for hp in range(2):
        for base in (o['prq'], o['prk']):
            cols += list(range(base + hp * 128, base + hp * 128 + 128))
            cols += [base + (hp * 2 + hh) * 64 + s for hh in range(2) for s in sw64]
    cols += list(range(o['prv'], o['prv'] + 256))
    cols += list(range(o['pglu'], o['pglu'] + 512))
    assert len(cols) == WIN_EXT
    w_in_ext = f(w_in[:, :, np.array(cols)])
    uq = inp['w_uq']
    ucols = []
    for h in range(8):
        ucols += list(range(h * 96, h * 96 + 96)) + list(h * 96 + 64 + sw32)
    w_uq_ext = f(uq[:, :, np.array(ucols)])
    kvc = [h * 128 + e for h in range(8) for e in range(64)] + [h * 128 + 64 + e for h in range(8) for e in range(64)]
    w_ukv_r = f(inp['w_ukv'][:, :, np.array(kvc)])
    pp = np.zeros((2, 128, NPP), np.float32)
    for l in range(2):
        pp[l, :, 0:24] = inp['b_mod'][l].reshape(24, 128).T
        pp[l, :, 24:26] = inp['mla_q_norm'][l].reshape(2, 128).T
        pp[l, :, 26] = inp['mla_kv_norm'][l]
        pp[l, :, 27:29] = inp['ret_gn_g'][l].reshape(2, 128).T
        pp[l, :, 29:31] = inp['ret_gn_b'][l].reshape(2, 128).T
        pp[l, :, 31:33] = inp['conv_dw_b'][l].reshape(2, 128).T
        pp[l, :, 33:35] = inp['conv_ln_g'][l].reshape(2, 128).T
        pp[l, :, 35:37] = inp['conv_ln_b'][l].reshape(2, 128).T
        pp[l, :, 37:39] = inp['conv_pw_b'][l].reshape(2, 128).T
        dw = inp['conv_dw'][l]
        pp[l, :, 39:101] = dw.T.reshape(2, 128, 31).transpose(1, 0, 2).reshape(128, 62)
    tab, ropeM, ropeRC, ropeRS = _consts()
    return dict(w_mod=f(inp['w_mod']), b_gate=f(inp['b_mod'][:, 2048:3072]), w_in=w_in_ext, w_uq=w_uq_ext, w_ukv=w_ukv_r,
                dec=f(np.concatenate([inp['ret_decay_fwd'], inp['ret_decay_bwd']], 1)), pw=f(inp['conv_pw']),
                w_out=f(inp['w_out']), ln_g=f(inp['ln_g']), ln_b=f(inp['ln_b']), pp=pp,
                ropeM=ropeM, ropeRC=ropeRC, ropeRS=ropeRS, tab=tab)


def _prep_core(inp, b):
    xin = np.ascontiguousarray(np.concatenate([inp['ctx'][b], inp['x'][b]], 0), dtype=np.float32)
    cc = np.stack([inp['c'][b].reshape(8, 128).T, inp['c_ctx'].reshape(8, 128).T], -1)
    return dict(xin=xin, cc=np.ascontiguousarray(cc.reshape(128, 16), dtype=np.float32))


_CACHE = {}


def kernel(**inputs):
    inp = {k: np.asarray(v) for k, v in inputs.items()}
    if 'nc' not in _CACHE:
        _CACHE['nc'] = build()[0]
    nc = _CACHE['nc']
    shared = _prep_shared(inp)
    in_maps = []
    for b in range(8):
        m = dict(shared)
        m.update(_prep_core(inp, b))
        in_maps.append(m)
    res = run_bass_kernel_spmd(nc, in_maps, core_ids=list(range(8)))
    out = np.stack([np.asarray(res.results[b]["out"]) for b in range(8)], 0)
    return out.astype(np.float32, copy=False)
```

```python
import contextlib
import numpy as np
import concourse.bass as bass
import concourse.mybir as mybir
from concourse.bass_utils import run_bass_kernel_spmd

F32 = mybir.dt.float32
BF16 = mybir.dt.bfloat16
AF = mybir.ActivationFunctionType
ALU = mybir.AluOpType
ENGS = ['pe', 'act', 'dve', 'pool', 'sp']
N_DSEM = 24
SBUF_TRACE = False
import os
CUT = int(os.environ.get('CUT', '0'))

D = 1024; L = 2048; CT = 256; T = CT + L; NT = T // 128; DEPTH = 2
EPS = 1e-5
ALPHA = (2 * DEPTH) ** 0.25
BLOCKS = [(0, 256), (256, 512), (768, 512), (1280, 512), (1792, 512)]
NPP = 24 + 2 + 1 + 2 + 2 + 2 + 2 + 2 + 2 + 62
WIN_EXT = 3264
UOFF_C = 15; UOFF_L = 15 + 256 + 30; UW = UOFF_L + 2048 + 15


class Res:
    __slots__ = ('name', 'w', 'r')

    def __init__(self, name=''):
        self.name = name
        self.w = None
        self.r = {}


class Prog:
    def __init__(self, nc, stack):
        self.nc = nc
        self.ops = {e: [] for e in ENGS}
        self.sem = {e: stack.enter_context(nc.semaphore('tk_' + e)) for e in ENGS}
        self.cnt = {e: 0 for e in ENGS}
        self.seen = {}
        self.dsem = [stack.enter_context(nc.semaphore('dq%d' % i)) for i in range(N_DSEM)]
        self.dcnt = [0] * N_DSEM
        self.dnext = 0
        self.out_tickets = []
        self.pending = {e: [] for e in ENGS}

    def _need(self, eng, t, waits):
        if t is None:
            return
        key, sem, val = t
        if key == eng and eng == 'pe':
            return
        if self.seen.get((eng, key), 0) >= val:
            return
        self.seen[(eng, key)] = val
        waits.append((sem, val))

    def _deps(self, eng, reads, writes):
        waits = self.pending[eng]
        self.pending[eng] = []
        for r in reads:
            self._need(eng, r.w, waits)
        for w in writes:
            self._need(eng, w.w, waits)
            for k, t in w.r.items():
                self._need(eng, t, waits)
        return waits

    def op(self, eng, fn, reads=(), writes=()):
        waits = self._deps(eng, reads, writes)
        self.cnt[eng] += 1
        t = (eng, self.sem[eng], self.cnt[eng])
        self.ops[eng].append((waits, fn, (self.sem[eng], 1)))
        for r in reads:
            r.r[eng] = t
        for w in writes:
            w.w = t
            w.r = {}
        return t

    def dma(self, eng, fn, reads=(), writes=(), is_output=False):
        i = self.dnext
        self.dnext = (self.dnext + 1) % N_DSEM
        waits = self._deps(eng, reads, writes)
        key = 'd%d' % i
        if self.dcnt[i] > 0:
            self._need(eng, (key, self.dsem[i], self.dcnt[i]), waits)
        self.dcnt[i] += 16
        t = (key, self.dsem[i], self.dcnt[i])
        self.ops[eng].append((waits, fn, (self.dsem[i], 16)))
        for r in reads:
            r.r[key] = t
        for w in writes:
            w.w = t
            w.r = {}
        if is_output:
            self.out_tickets.append(t)
        return t

    def barrier(self):
        for e in ENGS:
            for f in ENGS:
                if f != e and self.cnt[f] > 0:
                    self._need(e, (f, self.sem[f], self.cnt[f]), self.pending[e])
            for i in range(N_DSEM):
                if self.dcnt[i] > 0:
                    self._need(e, ('d%d' % i, self.dsem[i], self.dcnt[i]), self.pending[e])

    def emit(self):
        nc = self.nc
        fin = []
        for t in self.out_tickets:
            self._need('sp', t, fin)
        with nc.Block() as block:
            def run(e, engobj):
                for waits, fn, inc in self.ops[e]:
                    for s, v in waits:
                        engobj.wait_ge(s, v)
                    fn(engobj).then_inc(inc[0], inc[1])
                for s, v in self.pending[e]:
                    engobj.wait_ge(s, v)
                if e == 'sp':
                    for s, v in fin:
                        engobj.wait_ge(s, v)

            @block.tensor
            def _(eng):
                run('pe', eng)

            @block.scalar
            def _(eng):
                run('act', eng)

            @block.vector
            def _(eng):
                run('dve', eng)

            @block.gpsimd
            def _(eng):
                run('pool', eng)

            @block.sync
            def _(eng):
                run('sp', eng)


def bc(a, pos, n):
    ap = [list(x) for x in a.ap]
    ap.insert(pos, [0, n])
    return bass.AP(a.tensor, a.offset, ap)


def colbc(a, n):
    return bass.AP(a.tensor, a.offset, [list(a.ap[0]), [0, n]])


def build(n_layers=DEPTH, dbg=None, stage=6):
    nc = bass.Bass("TRN2", target_bir_lowering=False)
    dt_in = lambda n, s: nc.dram_tensor(n, s, F32, kind="ExternalInput")
    xin_d = dt_in("xin", [T, D])
    cc_d = dt_in("cc", [128, 16])
    wmod_d = dt_in("w_mod", [2, 1024, 3072])
    bgate_d = dt_in("b_gate", [2, 1024])
    win_d = dt_in("w_in", [2, 1024, WIN_EXT])
    wuq_d = dt_in("w_uq", [2, 256, 1024])
    wukv_d = dt_in("w_ukv", [2, 128, 1024])
    dec_d = dt_in("dec", [2, 8])
    pw_d = dt_in("pw", [2, 256, 256])
    wout_d = dt_in("w_out", [2, 1024, 1024])
    lng_d = dt_in("ln_g", [2, 1024])
    lnb_d = dt_in("ln_b", [2, 1024])
    pp_d = dt_in("pp", [2, 128, NPP])
    ropeM_d = dt_in("ropeM", [128, T])
    ropeRC_d = dt_in("ropeRC", [128, T])
    ropeRS_d = dt_in("ropeRS", [128, T])
    tab_d = dt_in("tab", [128, 6 * 128 + 2])
    out_d = nc.dram_tensor("out", [L, D], F32, kind="ExternalOutput")
    x1_d = nc.dram_tensor("x1", [T, D], F32, kind="Internal")
    dbg_outs = {}

    with contextlib.ExitStack() as st:
        P = Prog(nc, st)

        uid = [0]

        live = [0, 0]

        def sbt(stack, name, shape, dt):
            uid[0] += 1
            nb = int(np.prod(shape[1:])) * (2 if dt == BF16 else 4)
            nb = (nb + 31) // 32 * 32
            live[0] += nb
            live[1] = max(live[1], live[0])
            stack.callback(lambda: live.__setitem__(0, live[0] - nb))
            if SBUF_TRACE:
                print("alloc", name, nb, "live", live[0])
            return stack.enter_context(nc.sbuf_tensor("sb%d_%s" % (uid[0], name), shape, dt))

        def op(eng, name, *a, reads=(), writes=(), **kw):
            P.op(eng, lambda e: getattr(e, name)(*a, **kw), reads, writes)

        def mm(out, lhsT, rhs, start, stop, reads, writes):
            P.op('pe', lambda e: e.matmul(out, lhsT, rhs, start=start, stop=stop), reads, writes)

        def tp(out, in_, ident, reads, writes):
            P.op('pe', lambda e: e.transpose(out, in_, ident), reads, writes)

        def dma(out, in_, reads=(), writes=(), is_output=False, eng='sp'):
            P.dma(eng, lambda e: e.dma_start(out=out, in_=in_), reads, writes, is_output)

        def dump(name, ap, reads):
            if dbg is None or name not in dbg:
                return
            d = nc.dram_tensor("dbg_" + name, list(ap.shape), ap.dtype, kind="ExternalOutput")
            dbg_outs[name] = d
            dma(d.ap(), ap, reads=reads, is_output=True)

        uT = sbt(st, "uT", [128, 8, T], BF16)
        catT = sbt(st, "catT", [128, 8, T], BF16)
        r_uTd = [Res('uTd%d' % i) for i in range(NT)]
        r_uTa = [Res('uTa%d' % i) for i in range(NT)]
        r_uT = [r for i in range(NT) for r in (r_uTd[i], r_uTa[i])]
        r_cat = [[Res('cat%d_%d' % (c, i)) for i in range(NT)] for c in range(8)]
        wst = [sbt(st, "wst%d" % i, [128, 8, 256], F32) for i in range(2)]
        r_wst = [Res('wst%d' % i) for i in range(2)]
        wbf = [sbt(st, "wbf%d" % i, [128, 8, 256], BF16) for i in range(2)]
        r_wbf = [Res('wbf%d' % i) for i in range(2)]
        wrot = [0]
        NRES = 4
        RES0 = NT - NRES
        xres = sbt(st, "xres", [128, NRES, D], F32)
        r_xres = [Res('xres%d' % i) for i in range(NRES)]
        wgate = [sbt(st, "wgate%d" % i, [128, 8, 256], BF16) for i in range(2)]
        r_wgate = [Res('wgate%d' % i) for i in range(2)]
        ident = sbt(st, "ident", [128, 128], BF16)
        identf = sbt(st, "identf", [128, 128], F32)
        r_ident = Res('ident')
        ones_bf = sbt(st, "ones_bf", [128, 128], BF16)
        ones_f = sbt(st, "ones_f", [128, 128], F32)
        r_ones = Res('ones')
        tab = sbt(st, "tab", [128, 6 * 128 + 2], F32)
        r_tab = Res('tab')
        cT = sbt(st, "cT", [128, 8, 2], F32)
        r_c = Res('c')
        pp = sbt(st, "pp", [128, NPP], F32)
        r_pp = Res('pp')
        modT = sbt(st, "modT", [128, 16, 2], F32)
        r_mod = Res('modT')
        gate = sbt(st, "gate", [128, 2, D], F32)
        r_gate = Res('gate')
        stat = sbt(st, "stat", [128, 4, 16], F32)
        r_stat = [Res('stat%d' % i) for i in range(4)]
        lg = sbt(st, "lg", [128, 8], F32)
        lgt = sbt(st, "lgt", [128, 4, 8], F32)
        r_lg = Res('lg')

        psA = st.enter_context(nc.psum_tensor("psA", [128, 1024], F32))
        r_A = [Res('psA0'), Res('psA1')]
        psS = [st.enter_context(nc.psum_tensor("psS%d" % i, [128, 512], F32)) for i in range(3)]
        r_S = [Res('psS%d' % i) for i in range(3)]
        psG = [st.enter_context(nc.psum_tensor("psG%d" % i, [128, 512], F32)) for i in range(3)]
        r_G = [Res('psG%d' % i) for i in range(3)]
        grot = [0]; srot = [0]

        def nextG():
            i = grot[0]; grot[0] = (i + 1) % 3
            return psG[i], r_G[i]

        def nextS():
            i = srot[0]; srot[0] = (i + 1) % 3
            return psS[i], r_S[i]

        def tiles(t0, n):
            return range(t0 // 128, (t0 + n) // 128)

        def uTres(t0, n):
            return [r for i in tiles(t0, n) for r in (r_uTd[i], r_uTa[i])]

        dma(tab[:], tab_d.ap(), writes=[r_tab])
        dma(cT[:].rearrange("p k j -> p (k j)"), cc_d.ap(), writes=[r_c])
        op('pool', 'memset', identf[:], 0.0, writes=[r_ident])
        P.op('pool', lambda e: e.affine_select(identf[:], identf[:], [[-1, 128]], ALU.not_equal, 1.0, base=0,
                                               channel_multiplier=1), reads=[r_ident], writes=[r_ident])
        op('dve', 'tensor_copy', ident[:], identf[:], reads=[r_ident], writes=[r_ident])
        permT = sbt(st, "permT", [128, 128], BF16)
        r_perm = Res('permT')
        for h_ in range(2):
            for (d0, s0) in ((0, 16), (16, 0), (32, 48), (48, 32)):
                op('dve', 'tensor_copy', permT[:, 64 * h_ + d0:64 * h_ + d0 + 16], ident[:, 64 * h_ + s0:64 * h_ + s0 + 16],
                   reads=[r_ident], writes=[r_perm])
        op('pool', 'memset', ones_bf[:], 1.0, writes=[r_ones])
        op('pool', 'memset', ones_f[:], 1.0, writes=[r_ones])
        op('act', 'activation', cT[:], cT[:], AF.Silu, reads=[r_c], writes=[r_c])
        def load_w(dh, base_off, row_stride, ncols, nk=8, mul_ap=None, mul_reads=(), cast_eng='dve'):
            b = wrot[0]; wrot[0] = 1 - b
            src = bass.AP(dh, base_off, [[row_stride, 128], [128 * row_stride, nk], [1, ncols]])
            dma(wst[b][:, 0:nk, 0:ncols], src, writes=[r_wst[b]])
            if mul_ap is None:
                op(cast_eng, 'tensor_copy', wbf[b][:, 0:nk, 0:ncols], wst[b][:, 0:nk, 0:ncols],
                   reads=[r_wst[b]], writes=[r_wbf[b]])
            else:
                op('dve', 'tensor_tensor', wbf[b][:, 0:nk, 0:ncols], wst[b][:, 0:nk, 0:ncols], mul_ap, ALU.mult,
                   reads=[r_wst[b]] + list(mul_reads), writes=[r_wbf[b]])
            return wbf[b], r_wbf[b]

        def load_w32(dh, base_off, row_stride, ncols):
            b = wrot[0]; wrot[0] = 1 - b
            src = bass.AP(dh, base_off, [[row_stride, 128], [128 * row_stride, 8], [1, ncols]])
            dma(wst[b][:, :, 0:ncols], src, writes=[r_wst[b]])
            return wst[b], r_wst[b]

        def proj_fm(w, rw, c0, M, blocks, consume, nk=8, src=None, src_res=None):
            for (t0, n) in blocks:
                ps, rp = nextG()
                for k in range(nk):
                    if src is None:
                        rhs = uT[:, k, t0:t0 + n]; rr = uTres(t0, n)
                    else:
                        rhs = src[:, k, t0:t0 + n]; rr = src_res(t0, n)
                    mm(ps[0:M, 0:n], w[:, k, c0:c0 + M], rhs, k == 0, k == nk - 1, [rw] + rr, [rp])
                consume(ps, rp, t0, n)

        def rstd_from_var(var_ap, out_ap, rr, tmp_ap):
            op('act', 'activation', tmp_ap, var_ap, AF.Ln, bias=EPS_AP[0:var_ap.shape[0], :], reads=[rr, r_tab], writes=[rr])
            op('act', 'activation', out_ap, tmp_ap, AF.Exp, scale=-0.5, reads=[rr], writes=[rr])

        EPS_AP = tab[:, 768:769]
        ONE_AP = tab[:, 769:770]

        wmod_pref = {}
        for l in range(n_layers):
            src_d = xin_d if l == 0 else x1_d
            need_ctx = (l < DEPTH - 1)
            q_blocks = BLOCKS if need_ctx else BLOCKS[1:]
            q_tiles = list(range(NT)) if need_ctx else list(range(2, NT))

            ph01 = contextlib.ExitStack()
            xin = [sbt(ph01, "xin%d" % i, [128, D], F32) for i in range(4)]
            r_xin = [Res('xin%d' % i) for i in range(4)]
            crep = sbt(ph01, "crep", [128, 2, 8, 128], F32)
            bgt = sbt(ph01, "bgt", [128, D], F32)
            modrow = sbt(ph01, "modrow", [2, 2 * D], F32); r_modrow = Res('modrow')
            r_bgt = Res('bgt')
            xn = [sbt(ph01, "xn%d" % i, [128, D], BF16) for i in range(2)]
            r_xn = [Res('xn%d' % i) for i in range(2)]
            for j in range(2):
                for k in range(8):
                    op('dve', 'tensor_copy', crep[:, j, k, :], colbc(cT[:, k, j:j + 1], 128), reads=[r_c], writes=[r_c])
            dma(pp[:], pp_d.ap()[l], writes=[r_pp])
            dma(bgt[:], bass.AP(bgate_d, l * D, [[0, 128], [1, D]]), writes=[r_bgt])
            dma(lg[:], bass.AP(dec_d, l * 8, [[0, 128], [1, 8]]), writes=[r_lg])
            op('act', 'activation', lgt[:, 0, :], lg[:], AF.Exp, scale=-1.0, reads=[r_lg], writes=[r_lg])
            op('dve', 'tensor_scalar', lgt[:, 1, :], lgt[:, 0, :], 1.0, None, ALU.add, reads=[r_lg], writes=[r_lg])
            op('act', 'activation', lgt[:, 2, :], lgt[:, 1, :], AF.Ln, reads=[r_lg], writes=[r_lg])
            op('dve', 'tensor_scalar', lgt[:, 3, :], lgt[:, 1, :], -1.0, 1e-30, ALU.add, ALU.max, reads=[r_lg], writes=[r_lg])
            op('dve', 'reciprocal', lgt[:, 3, :], lgt[:, 3, :], reads=[r_lg], writes=[r_lg])
            op('dve', 'tensor_tensor', lgt[:, 3, :], lgt[:, 3, :], lgt[:, 0, :], ALU.mult, reads=[r_lg], writes=[r_lg])
            op('dve', 'scalar_tensor_tensor', lg[:], lgt[:, 2, :], -1.0, lgt[:, 3, :], ALU.mult, ALU.mult,
               reads=[r_lg], writes=[r_lg])

            mod_ps, r_modps = psA[:, 0:512], r_A[0]
            mod_v = mod_ps[:, 0:32].rearrange("p (c j) -> p c j", j=2)

            def p0_chunk(cb):
                if cb < len(wmod_pref.get(l, [])):
                    w32, rw = wmod_pref[l][cb]
                else:
                    w32, rw = load_w32(wmod_d, l * 1024 * 3072 + cb * 256, 3072, 256)
                if cb < 8:
                    ps, rp = nextS()
                    for k in range(8):
                        mm(ps[0:2, 0:256], cT[:, k, :], w32[:, k, :], k == 0, k == 7, [rw, r_c], [rp])
                    op('dve', 'tensor_copy', modrow[0:2, cb * 256:(cb + 1) * 256], ps[0:2, 0:256], reads=[rp], writes=[r_modrow])
                    if cb == 7:
                        for ci in range(16):
                            tp(mod_v[:, ci, :], modrow[0:2, ci * 128:(ci + 1) * 128], identf[0:2, 0:2], [r_modrow, r_ident], [r_modps])
                else:
                    n0 = (cb - 8) * 256
                    for j in (range(2) if need_ctx else range(1)):
                        ps, rp = nextS()
                        for k in range(8):
                            mm(ps[:, 0:256], crep[:, j, k, :], w32[:, k, :], k == 0, k == 7, [rw, r_c], [rp])
                        op('dve', 'tensor_tensor', gate[:, j, n0:n0 + 256], ps[:, 0:256], bgt[:, n0:n0 + 256], ALU.add,
                           reads=[rp, r_bgt], writes=[r_gate])

            def xsrc1(i):
                if i >= RES0:
                    return xres[:, i - RES0, :], r_xres[i - RES0], (l == 0)
                return xin[i % 4][:], r_xin[i % 4], True

            def p1_stats(i):
                xa, rx, need = xsrc1(i)
                if need:
                    dma(xa, src_d.ap()[i * 128:(i + 1) * 128, :], writes=[rx])
                s = i % 4
                op('dve', 'bn_stats', stat[:, s, 0:6], xa[:, 0:512], reads=[rx], writes=[r_stat[s]])
                op('dve', 'bn_stats', stat[:, s, 6:12], xa[:, 512:1024], reads=[rx], writes=[r_stat[s]])
                op('dve', 'bn_aggr', stat[:, s, 12:14], stat[:, s, 0:12], reads=[r_stat[s]], writes=[r_stat[s]])
                rstd_from_var(stat[:, s, 13:14], stat[:, s, 14:15], r_stat[s], stat[:, s, 15:16])

            def p1_norm(i):
                s = i % 4; xb = i % 2
                xa, rx, _ = xsrc1(i)
                op('dve', 'tensor_scalar', xn[xb][:], xa, stat[:, s, 12:13], stat[:, s, 14:15], ALU.subtract, ALU.mult,
                   reads=[rx, r_stat[s]], writes=[r_xn[xb]])
                psa_, rpa_ = nextG()
                psb_, rpb_ = nextS()
                va_ = psa_[:, :].bitcast(BF16).rearrange("p (k n) -> p k n", k=8)
                vb_ = psb_[:, :].bitcast(BF16).rearrange("p (k n) -> p k n", k=8)
                for k in range(8):
                    if k % 2 == 0:
                        tp(va_[:, k // 2, :], xn[xb][:, k * 128:(k + 1) * 128], ident[:], [r_xn[xb], r_ident], [rpa_])
                    else:
                        tp(vb_[:, k // 2, :], xn[xb][:, k * 128:(k + 1) * 128], ident[:], [r_xn[xb], r_ident], [rpb_])
                op('dve', 'tensor_copy', uT[:, 0:8:2, i * 128:(i + 1) * 128], va_[:, 0:4, :], reads=[rpa_], writes=[r_uTd[i]])
                op('act', 'copy', uT[:, 1:8:2, i * 128:(i + 1) * 128], vb_[:, 0:4, :], reads=[rpb_], writes=[r_uTa[i]])

            p1_steps = [lambda: p1_stats(0)]
            for i in range(NT):
                def step(i=i):
                    if i + 1 < NT:
                        p1_stats(i + 1)
                    p1_norm(i)
                p1_steps.append(step)
            p1_steps[0]()
            done = 1
            for cb in range(12):
                p0_chunk(cb)
                want = 1 + ((cb + 1) * NT) // 12
                while done < want:
                    p1_steps[done](); done += 1
            while done < len(p1_steps):
                p1_steps[done](); done += 1
            op('dve', 'tensor_tensor', modT[:], mod_v[:, 0:16, :], bc(pp[:, 0:16], 2, 2), ALU.add,
               reads=[r_modps, r_pp], writes=[r_mod])
            op('dve', 'tensor_scalar', modT[:, 8:16, :], modT[:, 8:16, :], 1.0, None, ALU.add, reads=[r_mod], writes=[r_mod])
            dump('modT%d' % l, modT[:], [r_mod])
            dump('gate%d' % l, gate[:], [r_gate])
            for k in range(8):
                for (j, lo, hi) in ((1, 0, CT), (0, CT, T)):
                    rr = [r for i in tiles(lo, hi - lo) for r in (r_uTd[i], r_uTa[i])]
                    if k % 2 == 0:
                        op('dve', 'tensor_scalar', uT[:, k, lo:hi], uT[:, k, lo:hi], modT[:, 8 + k, j:j + 1], modT[:, k, j:j + 1],
                           ALU.mult, ALU.add, reads=rr + [r_mod], writes=rr)
                    else:
                        op('act', 'activation', uT[:, k, lo:hi], uT[:, k, lo:hi], AF.Identity, bias=modT[:, k, j:j + 1], scale=modT[:, 8 + k, j:j + 1],
                           reads=rr + [r_mod], writes=rr)
            dump('uT%d' % l, uT[:], r_uT)
            WOFF = l * 1024 * WIN_EXT
            w_g2, rw_g2 = load_w(win_d, WOFF + 2 * 256, WIN_EXT, 256)
            w_rv, rw_rv = load_w(win_d, WOFF + 2496, WIN_EXT, 256)
            P.barrier()
            ph01.close()

            w, rw = w_g2, rw_g2
            for m in range(2):
                c = 4 + m

                def cons(ps, rp, t0, n, c=c):
                    op('act', 'activation', catT[:, c, t0:t0 + n], ps[:, 0:n], AF.Silu, reads=[rp],
                       writes=[r_cat[c][i] for i in tiles(t0, n)])
                proj_fm(w, rw, m * 128, 128, q_blocks, cons)
            order_g = [3, 0, 1]
            p2_units = [(gi, g, blk, bi == 0) for gi, g in enumerate(order_g) for bi, blk in enumerate(q_blocks)]
            p2_state = {'next': 0, 'loaded': 0}

            def load_wg(g, slot):
                b = wrot[0]; wrot[0] = 1 - b
                src = bass.AP(win_d, WOFF + g * 256, [[WIN_EXT, 128], [128 * WIN_EXT, 8], [1, 256]])
                dma(wst[b][:, :, :], src, writes=[r_wst[b]])
                op('dve', 'tensor_copy', wgate[slot][:], wst[b][:], reads=[r_wst[b]], writes=[r_wgate[slot]])

            def p2_tick(nmax=1):
                for _ in range(nmax):
                    if p2_state['next'] >= len(p2_units):
                        return
                    gi, g, (t0, n), first = p2_units[p2_state['next']]
                    p2_state['next'] += 1
                    slot = gi % 2
                    if first:
                        while p2_state['loaded'] <= min(gi + 1, len(order_g) - 1):
                            load_wg(order_g[p2_state['loaded']], p2_state['loaded'] % 2)
                            p2_state['loaded'] += 1
                    for m in range(2):
                        for k in range(8):
                            mm(psA[:, m * 512:m * 512 + n], wgate[slot][:, k, m * 128:(m + 1) * 128], uT[:, k, t0:t0 + n], k == 0, k == 7,
                               [r_wgate[slot]] + uTres(t0, n), [r_A[m]])
                    for m in range(2):
                        c = g * 2 + m
                        op('act', 'activation', catT[:, c, t0:t0 + n], psA[:, m * 512:m * 512 + n], AF.Silu, reads=[r_A[m]],
                           writes=[r_cat[c][i] for i in tiles(t0, n)])

            WOFF = l * 1024 * WIN_EXT
            if stage < 3:
                continue
            with contextlib.ExitStack() as ph:
                rq = sbt(ph, "rq", [128, T], BF16); rk = sbt(ph, "rk", [128, T], BF16)
                r_rq = [Res() for _ in range(NT)]; r_rk = [Res() for _ in range(NT)]
                QD = [sbt(ph, "QD%d" % h, [128, T], BF16) for h in range(2)]
                r_QD = [[Res() for _ in range(NT)] for h in range(2)]
                kf = sbt(ph, "kf", [128, NT, 128], BF16); kb = sbt(ph, "kb", [128, NT, 128], BF16)
                r_kfb = [Res() for _ in range(NT)]
                rv = sbt(ph, "rv", [128, NT, 256], BF16)
                r_rv = [Res() for _ in range(NT)]
                Scat = [sbt(ph, "Scat%d" % h, [128, NT, 64], BF16) for h in range(2)]
                r_Scat = [[Res() for _ in range(NT)] for h in range(2)]
                ST = sbt(ph, "ST", [128, 2, 128], F32); r_ST = [Res(), Res()]
                RC = sbt(ph, "RC", [128, T], BF16); RS = sbt(ph, "RS", [128, T], BF16); r_rt = Res()
                DcP = [sbt(ph, "Dc%d" % i, [128, 2, 128], F32) for i in range(2)]; DtP = [sbt(ph, "Dt%d" % i, [128, 2, 128], F32) for i in range(2)]; r_DcP = [Res(), Res()]
                QTP = [sbt(ph, "QT%d" % i, [128, 2, 2, 128], F32) for i in range(2)]; r_QTP = [Res(), Res()]
                KFP = [sbt(ph, "KFt%d" % i, [128, 2, 128], F32) for i in range(2)]; r_KFP = [Res(), Res()]
                gpP = [sbt(ph, "gp%d" % i, [128, 2], F32) for i in range(2)]; r_gpP = [Res(), Res()]
                rt1 = [sbt(ph, "rt1_%d" % i, [128, 512], F32) for i in range(2)]; r_rt1 = [Res(), Res()]
                qsb = [sbt(ph, "qsb%d" % i, [128, 512], BF16) for i in range(2)]; r_qsb = [Res(), Res()]
                rt2 = [sbt(ph, "rt2_%d" % i, [128, 512], F32) for i in range(2)]; r_rt2 = [Res(), Res()]
                PT2 = [sbt(ph, "PT2_%d" % i, [128, 2, 128], BF16) for i in range(3)]; r_PT2 = [Res() for _ in range(3)]
                gst = sbt(ph, "gst", [128, 4, 32], F32); r_gst = [Res() for _ in range(4)]
                on = [sbt(ph, "on%d" % i, [128, 128], BF16) for i in range(2)]; r_on = [Res(), Res()]
                ot = [sbt(ph, "ot%d" % i, [128, 128], BF16) for i in range(2)]; r_ot = [Res(), Res()]
                for bi_, (t0, n) in enumerate(BLOCKS):
                    for tb_, (dst_, src_) in enumerate(((RC, ropeRC_d), (RS, ropeRS_d))):
                        stg, rstg = (rt1[bi_ % 2], r_rt1[bi_ % 2]) if tb_ == 0 else (rt2[bi_ % 2], r_rt2[bi_ % 2])
                        dma(stg[:, 0:n], src_.ap()[:, t0:t0 + n], writes=[rstg])
                        op('dve', 'tensor_copy', dst_[:, t0:t0 + n], stg[:, 0:n], reads=[rstg], writes=[r_rt])
                def build_tables(hp):
                    Dt = DtP[hp]
                    Dc, r_Dc = DcP[hp], r_DcP[hp]; QT, r_QT = QTP[hp], r_QTP[hp]; KFt, r_KF = KFP[hp], r_KFP[hp]; gp, r_gp = gpP[hp], r_gpP[hp]
                    for hh in range(2):
                        h = hp * 2 + hh
                        op('act', 'activation', Dc[:, hh, :], tab[:, 0:128], AF.Exp, scale=lg[:, h:h + 1], reads=[r_tab, r_lg], writes=[r_Dc])
                        op('act', 'activation', Dt[:, hh, :], tab[:, 256:384], AF.Exp, scale=lg[:, 4 + h:5 + h], reads=[r_tab, r_lg], writes=[r_Dc])
                        op('dve', 'tensor_tensor', Dc[:, hh, :], Dc[:, hh, :], tab[:, 128:256], ALU.mult, reads=[r_Dc, r_tab], writes=[r_Dc])
                        op('dve', 'tensor_tensor', Dt[:, hh, :], Dt[:, hh, :], tab[:, 384:512], ALU.mult, reads=[r_Dc, r_tab], writes=[r_Dc])
                        op('dve', 'tensor_tensor', Dc[:, hh, :], Dc[:, hh, :], Dt[:, hh, :], ALU.add, reads=[r_Dc], writes=[r_Dc])
                        op('act', 'activation', QT[:, hh, 0, :], tab[:, 512:640], AF.Exp, scale=lg[:, h:h + 1], reads=[r_tab, r_lg], writes=[r_QT])
                        op('act', 'activation', QT[:, hh, 1, :], tab[:, 640:768], AF.Exp, scale=lg[:, 4 + h:5 + h], reads=[r_tab, r_lg], writes=[r_QT])
                        op('act', 'activation', KFt[:, 0, hh * 64:(hh + 1) * 64], colbc(tab[:, 127:128], 64),
                           AF.Exp, scale=lg[:, h:h + 1], reads=[r_tab, r_lg], writes=[r_KF])
                        op('act', 'activation', KFt[:, 1, hh * 64:(hh + 1) * 64], colbc(tab[:, 256:257], 64),
                           AF.Exp, scale=lg[:, 4 + h:5 + h], reads=[r_tab, r_lg], writes=[r_KF])
                        sl = slice(hh * 64, (hh + 1) * 64)
                        op('act', 'activation', gp[sl, 0:1], ONE_AP[sl, :], AF.Exp, scale=lg[sl, h:h + 1], reads=[r_tab, r_lg], writes=[r_gp])
                        op('act', 'activation', gp[sl, 1:2], ONE_AP[sl, :], AF.Exp, scale=lg[sl, 4 + h:5 + h], reads=[r_tab, r_lg], writes=[r_gp])
                    for _ in range(7):
                        op('dve', 'tensor_tensor', gp[:], gp[:], gp[:], ALU.mult, reads=[r_gp], writes=[r_gp])

                for hp_ in range(2):
                    build_tables(hp_)
                w, rw = w_rv, rw_rv
                for i in range(NT):
                    ps, rp = nextG()
                    for k in range(8):
                        mm(ps[:, 0:256], uT[:, k, i * 128:(i + 1) * 128], w[:, k, 0:256], k == 0, k == 7, [rw, r_uTd[i], r_uTa[i]], [rp])
                    op('act', 'copy', rv[:, i, :], ps[:, 0:256], reads=[rp], writes=[r_rv[i]])
                while p2_state['loaded'] < min(2, len(order_g)):
                    load_wg(order_g[p2_state['loaded']], p2_state['loaded'] % 2)
                    p2_state['loaded'] += 1
                prot = [0]
                for hp in range(2):
                    Dc, r_Dc = DcP[hp], r_DcP[hp]; QT, r_QT = QTP[hp], r_QTP[hp]; KFt, r_KF = KFP[hp], r_KFP[hp]; gp, r_gp = gpP[hp], r_gpP[hp]
                    wq_, rwq_ = load_w(win_d, WOFF + 1472 + hp * 512, WIN_EXT, 128)
                    wk_, rwk_ = load_w(win_d, WOFF + 1472 + hp * 512 + 256, WIN_EXT, 128)
                    pitems = [(wq_, rwq_, rq, r_rq, 1.0, blk) for blk in BLOCKS] + [(wk_, rwk_, rk, r_rk, 0.125, blk) for blk in BLOCKS]
                    pheld = {}

                    def pj_a(ii):
                        w, rw, dst, r_dst, sc, (t0, n) = pitems[ii]
                        psa, rpa = nextG()
                        for k in range(8):
                            mm(psa[:, 0:n], w[:, k, 0:128], uT[:, k, t0:t0 + n], k == 0, k == 7, [rw] + uTres(t0, n), [rpa])
                        pheld[ii] = (psa, rpa)

                    def pj_b(ii):
                        w, rw, dst, r_dst, sc, (t0, n) = pitems[ii]
                        psa, rpa = pheld.pop(ii)
                        b = ii % 2
                        op('dve', 'tensor_copy', qsb[b][:, 0:n], psa[:, 0:n], reads=[rpa], writes=[r_qsb[b]])
                        psb_, rpb = nextS()
                        mm(psb_[:, 0:n], permT[:], qsb[b][:, 0:n], True, True, [r_perm, r_qsb[b]], [rpb])
                        op('dve', 'scalar_tensor_tensor', rt1[b][:, 0:n], psa[:, 0:n], sc, RC[:, t0:t0 + n], ALU.mult, ALU.mult,
                           reads=[rpa, r_rt], writes=[r_rt1[b]])
                        op('dve', 'scalar_tensor_tensor', rt2[b][:, 0:n], psb_[:, 0:n], sc, RS[:, t0:t0 + n], ALU.mult, ALU.mult,
                           reads=[rpb, r_rt], writes=[r_rt2[b]])
                        op('pool', 'tensor_tensor', dst[:, t0:t0 + n], rt1[b][:, 0:n], rt2[b][:, 0:n], ALU.add,
                           reads=[r_rt1[b], r_rt2[b]], writes=[r_dst[i] for i in tiles(t0, n)])

                    pj_a(0)
                    for ii in range(len(pitems)):
                        if ii + 1 < len(pitems):
                            pj_a(ii + 1)
                        pj_b(ii)
                    dump('rq%d_%d' % (l, hp), rq[:], r_rq)
                    dump('rk%d_%d' % (l, hp), rk[:], r_rk)
                    for hh in range(2):
                        src_rows = slice(hh * 64, (hh + 1) * 64)
                        for (t0, n) in BLOCKS:
                            nb = n // 128
                            for d in range(2):
                                drows = slice(d * 64, (d + 1) * 64)
                                eng = 'pool' if d == hh else 'dve'
                                op(eng, 'tensor_tensor', QD[hh][drows, t0:t0 + n].rearrange("p (c i) -> p c i", i=128),
                                   rq[src_rows, t0:t0 + n].rearrange("p (c i) -> p c i", i=128), bc(QT[src_rows, hh, d, :], 1, nb), ALU.mult,
                                   reads=[r_rq[i] for i in tiles(t0, n)] + [r_QT], writes=[r_QD[hh][i] for i in tiles(t0, n)])
                    p2_tick(5)
                    for i in range(NT):
                        ps, rp = nextG()
                        psb = ps[:, :].bitcast(BF16)
                        tp(psb[:, 0:128], rk[:, i * 128:(i + 1) * 128], ident[:], [r_rk[i], r_ident], [rp])
                        op('dve', 'tensor_tensor', kf[:, i, :], psb[:, 0:128], KFt[:, 0, :], ALU.mult, reads=[rp, r_KF], writes=[r_kfb[i]])
                        op('dve', 'tensor_tensor', kb[:, i, :], psb[:, 0:128], KFt[:, 1, :], ALU.mult, reads=[rp, r_KF], writes=[r_kfb[i]])
                    order_f = list(range(NT))
                    order_b = list(range(2, NT)) + [0, 1]
                    dirs = ((0, order_f, kf), (1, order_b[::-1], kb))
                    for d in range(2):
                        op('dve', 'memset', ST[:, d, :], 0.0, writes=[r_ST[d]])
                    for pos in range(NT):
                        if pos % 8 == 3 and CUT != 1:
                            p2_tick(1)
                        for d, order, ksrc in dirs:
                            c = order[pos]
                            for hh in range(2):
                                drows = slice(d * 64, (d + 1) * 64)
                                op('act' if hh == d else 'dve', 'copy' if hh == d else 'tensor_copy', Scat[hh][drows, c, :], ST[hh * 64:(hh + 1) * 64, d, hh * 64:(hh + 1) * 64],
                                   reads=[r_ST[d]], writes=[r_Scat[hh][c]])
                            if pos == NT - 1:
                                continue
                            ps, rp = nextG()
                            mm(ps[:, 0:128], ksrc[:, c, :], rv[:, c, hp * 128:(hp + 1) * 128], True, True, [r_kfb[c], r_rv[c]], [rp])
                            op('dve', 'scalar_tensor_tensor', ST[:, d, :], ST[:, d, :], gp[:, d:d + 1], ps[:, 0:128], ALU.mult, ALU.add,
                               reads=[r_ST[d], r_gp, rp], writes=[r_ST[d]])
                    held = {}
                    psa_rot = [0]

                    psa_pool = [(psS[0], r_S[0]), (psS[1], r_S[1]), (psS[2], r_S[2]), (psA[:, 0:512], r_A[0])]

                    def st_a1(c):
                        cs = slice(c * 128, (c + 1) * 128)
                        got = []
                        for hh in range(2):
                            rows = slice(hh * 64, (hh + 1) * 64)
                            psa, rpa = psa_pool[psa_rot[0]]; psa_rot[0] = (psa_rot[0] + 1) % 4
                            mm(psa[:, 0:128], rk[rows, cs], rq[rows, cs], True, True, [r_rk[c], r_rq[c]], [rpa])
                            got.append((psa, rpa))
                        held[c] = got

                    def st_a2(c):
                        cs = slice(c * 128, (c + 1) * 128)
                        got = held[c]
                        pi = c % 3
                        for hh in range(2):
                            psa, rpa = got[hh]
                            op('dve', 'tensor_tensor', PT2[pi][:, hh, :], psa[:, 0:128], Dc[:, hh, :], ALU.mult,
                               reads=[rpa, r_Dc], writes=[r_PT2[pi]])
                        ops_, rpo = nextG()
                        for hh in range(2):
                            mm(ops_[:, hh:128:2], PT2[pi][:, hh, :], rv[:, c, (hp * 2 + hh) * 64:(hp * 2 + hh + 1) * 64], True, False,
                               [r_PT2[pi], r_rv[c]], [rpo])
                            mm(ops_[:, hh:128:2], QD[hh][:, cs], Scat[hh][:, c, :], False, True,
                               [r_QD[hh][c], r_Scat[hh][c]], [rpo])
                        held[c] = [ops_, rpo]

                    def st_b1(c):
                        ops_, rpo = held[c]
                        s = c % 4
                        op('dve', 'bn_stats', gst[:, s, 0:6], ops_[:, 0:128], reads=[rpo], writes=[r_gst[s]])
                        op('act', 'activation', gst[:, s, 18:20], gst[:, s, 2:6:3], AF.Ln, bias=EPS_AP, scale=1.0 / 64, reads=[r_gst[s], r_tab], writes=[r_gst[s]])
                        op('act', 'activation', gst[:, s, 16:18], gst[:, s, 18:20], AF.Exp, scale=-0.5, reads=[r_gst[s]], writes=[r_gst[s]])

                    def st_b2a(c):
                        ops_, rpo = held[c]
                        s = c % 4; b2 = c % 2
                        op('dve', 'scalar_tensor_tensor', gst[:, s, 20:22], gst[:, s, 1:5:3], -1.0, gst[:, s, 16:18], ALU.mult, ALU.mult,
                           reads=[r_gst[s]], writes=[r_gst[s]])
                        for hh in range(2):
                            op('act', 'activation', on[b2][:, hh * 64:(hh + 1) * 64], ops_[:, hh:128:2], AF.Identity,
                               bias=gst[:, s, 20 + hh:21 + hh], scale=gst[:, s, 16 + hh:17 + hh], reads=[rpo, r_gst[s]], writes=[r_on[b2]])

                    def st_b2b(c):
                        cs = slice(c * 128, (c + 1) * 128)
                        held.pop(c)
                        b2 = c % 2
                        pst, rpt = psA[:, 512:1024], r_A[1]
                        pstb = pst.bitcast(BF16)
                        tp(pstb[:, 0:128], on[b2][:], ident[:], [r_on[b2], r_ident], [rpt])
                        op('act', 'activation', ot[b2][:], pstb[:, 0:128], AF.Identity, bias=pp[:, 29 + hp:30 + hp], scale=pp[:, 27 + hp:28 + hp],
                           reads=[rpt, r_pp], writes=[r_ot[b2]])
                        op('pool', 'tensor_tensor', catT[:, 4 + hp, cs], ot[b2][:], catT[:, 4 + hp, cs], ALU.mult,
                           reads=[r_ot[b2], r_cat[4 + hp][c]], writes=[r_cat[4 + hp][c]])

                    stages_ = [st_a1, st_a2, st_b1, st_b2a, st_b2b]
                    nqt = len(q_tiles)
                    for t_ in range(nqt + len(stages_) - 1):
                        for si_, fn_ in enumerate(stages_):
                            ix_ = t_ - si_
                            if 0 <= ix_ < nqt:
                                fn_(q_tiles[ix_])
                p2_tick(999)
                wg, rwg = load_w(win_d, WOFF + 3008, WIN_EXT, 256)
                wa, rwa = load_w(win_d, WOFF + 2752, WIN_EXT, 256)
                dump('catret%d' % l, catT[:, 4:6, :], r_cat[4] + r_cat[5])
                P.barrier()

            phm = contextlib.ExitStack()
            RM = sbt(phm, "RM", [128, T], F32); r_RM = Res()
            wuqb = sbt(phm, "wuqb", [128, 2, 1024], BF16); r_wuq = Res()
            wukvb = sbt(phm, "wukvb", [128, 1024], BF16); r_wukv = Res()
            wuqf = wst[0][:].rearrange("p k n -> p (k n)").rearrange("p (k n) -> p k n", k=2)
            wukvf = wst[1][:].rearrange("p k n -> p (k n)")[:, 0:1024]
            dma(RM[64:128, :], ropeM_d.ap()[64:128, :], writes=[r_RM])
            dma(wuqf, bass.AP(wuq_d, l * 256 * 1024, [[1024, 128], [128 * 1024, 2], [1, 1024]]), writes=[r_wst[0]])
            dma(wukvf, wukv_d.ap()[l], writes=[r_wst[1]])
            for k in range(2):
                op('dve', 'tensor_scalar', wuqb[:, k, :], wuqf[:, k, :], pp[:, 24 + k:25 + k], None, ALU.mult, reads=[r_wst[0], r_pp], writes=[r_wuq])
            op('dve', 'tensor_scalar', wukvb[:], wukvf, pp[:, 26:27], None, ALU.mult, reads=[r_wst[1], r_pp], writes=[r_wukv])

            if stage < 4:
                continue
            with contextlib.ExitStack() as ph:
                U = sbt(ph, "U", [128, 2, UW], BF16); r_U = Res()
                diag = sbt(ph, "diag", [128, 62, 128], BF16); r_diag = Res()
                sig = [sbt(ph, "sig%d" % i, [128, 512], F32) for i in range(2)]; r_sig = [Res(), Res()]
                Y = [sbt(ph, "Y%d" % i, [128, 2, 512], F32) for i in range(2)]; r_Y = [Res(), Res()]
                Y2 = [sbt(ph, "Y2_%d" % i, [128, 2, 512], BF16) for i in range(2)]; r_Y2 = [Res(), Res()]
                Yb = [sbt(ph, "Yb_%d" % i, [128, 2, 512], BF16) for i in range(2)]; r_Yb = [Res(), Res()]
                cs_ = [sbt(ph, "cst0", [128, 3, 512], F32)] * 2; r_cs = [Res()] * 2
                Z = [sbt(ph, "Z%d" % i, [128, 2, 512], BF16) for i in range(2)]; r_Z = [Res(), Res()]
                pwb = sbt(ph, "pwb", [128, 2, 256], BF16); pwf = sbt(ph, "pwf", [128, 2, 256], F32); r_pw = Res()
                op('pool', 'memset', U[:], 0.0, writes=[r_U])
                diag_todo = [(ch, kk) for ch in range(2) for kk in range(31)]

                def diag_tick(nmax):
                    for _ in range(nmax):
                        if not diag_todo:
                            return
                        ch, kk = diag_todo.pop(0)
                        op('dve', 'tensor_scalar', diag[:, ch * 31 + kk, :], ident[:], pp[:, 39 + ch * 31 + kk:40 + ch * 31 + kk], None, ALU.mult,
                           reads=[r_ident, r_pp], writes=[r_diag])
                dma(pwf[:], bass.AP(pw_d, l * 256 * 256, [[256, 128], [128 * 256, 2], [1, 256]]), writes=[r_pw])
                op('pool', 'tensor_copy', pwb[:], pwf[:], reads=[r_pw], writes=[r_pw])
                srot_ = [0]
                for ch in range(2):
                    for (t0, n) in q_blocks:
                        psg, rpg = nextG()
                        for k in range(8):
                            mm(psg[:, 0:n], wg[:, k, ch * 128:(ch + 1) * 128], uT[:, k, t0:t0 + n], k == 0, k == 7, [rwg] + uTres(t0, n), [rpg])
                        psa, rpa = nextG()
                        for k in range(8):
                            mm(psa[:, 0:n], wa[:, k, ch * 128:(ch + 1) * 128], uT[:, k, t0:t0 + n], k == 0, k == 7, [rwa] + uTres(t0, n), [rpa])
                        b = srot_[0]; srot_[0] = 1 - b
                        op('act', 'activation', sig[b][:, 0:n], psg[:, 0:n], AF.Sigmoid, reads=[rpg], writes=[r_sig[b]])
                        u0 = (UOFF_C + t0) if t0 < CT else (UOFF_L + t0 - CT)
                        op('dve', 'tensor_tensor', U[:, ch, u0:u0 + n], psa[:, 0:n], sig[b][:, 0:n], ALU.mult, reads=[rpa, r_sig[b]], writes=[r_U])
                        diag_tick(8)
                diag_tick(99)

                def conv_dw(bi):
                    t0, n = q_blocks[bi]; b = bi % 2
                    u0 = (UOFF_C + t0) if t0 < CT else (UOFF_L + t0 - CT)
                    for ch in range(2):
                        ps, rp = nextS()
                        for kk in range(31):
                            mm(ps[:, 0:n], diag[:, ch * 31 + kk, :], U[:, ch, u0 + kk - 15:u0 + kk - 15 + n], kk == 0, kk == 30, [r_diag, r_U], [rp])
                        op('dve', 'tensor_scalar', Y[b][:, ch, 0:n], ps[:, 0:n], pp[:, 31 + ch:32 + ch], None, ALU.add, reads=[rp, r_pp], writes=[r_Y[b]])
                        op('pool', 'tensor_tensor', Y2[b][:, ch, 0:n], Y[b][:, ch, 0:n], Y[b][:, ch, 0:n], ALU.mult, reads=[r_Y[b]], writes=[r_Y2[b]])
                        op('act', 'copy', Yb[b][:, ch, 0:n], Y[b][:, ch, 0:n], reads=[r_Y[b]], writes=[r_Yb[b]])

                def conv_post(bi):
                    t0, n = q_blocks[bi]; b = bi % 2
                    ps1, rp1 = nextG()
                    for ch in range(2):
                        mm(ps1[:, 0:n], ones_bf[:], Yb[b][:, ch, 0:n], ch == 0, ch == 1, [r_ones, r_Yb[b]], [rp1])
                    ps2, rp2 = nextG()
                    for ch in range(2):
                        mm(ps2[:, 0:n], ones_bf[:], Y2[b][:, ch, 0:n], ch == 0, ch == 1, [r_ones, r_Y2[b]], [rp2])
                    mean = cs_[b][:, 0, 0:n]; var = cs_[b][:, 1, 0:n]; rstd = cs_[b][:, 1, 0:n]; tmp = cs_[b][:, 2, 0:n]
                    op('dve', 'tensor_scalar', mean, ps1[:, 0:n], 1.0 / 256, None, ALU.mult, reads=[rp1], writes=[r_cs[b]])
                    op('dve', 'tensor_tensor', var, mean, mean, ALU.mult, reads=[r_cs[b]], writes=[r_cs[b]])
                    op('dve', 'scalar_tensor_tensor', var, ps2[:, 0:n], 1.0 / 256, var, ALU.mult, ALU.subtract, reads=[rp2, r_cs[b]], writes=[r_cs[b]])
                    op('act', 'activation', tmp, var, AF.Ln, bias=EPS_AP, reads=[r_cs[b], r_tab], writes=[r_cs[b]])
                    op('act', 'activation', rstd, tmp, AF.Exp, scale=-0.5, reads=[r_cs[b]], writes=[r_cs[b]])
                    for ch in range(2):
                        op('dve', 'tensor_tensor', Y[b][:, ch, 0:n], Y[b][:, ch, 0:n], mean, ALU.subtract, reads=[r_Y[b], r_cs[b]], writes=[r_Y[b]])
                        op('dve', 'tensor_tensor', Y[b][:, ch, 0:n], Y[b][:, ch, 0:n], rstd, ALU.mult, reads=[r_Y[b], r_cs[b]], writes=[r_Y[b]])
                    for ch in range(2):
                        op('act', 'activation', Z[b][:, ch, 0:n], Y[b][:, ch, 0:n], AF.Silu, bias=pp[:, 35 + ch:36 + ch], scale=pp[:, 33 + ch:34 + ch],
                           reads=[r_Y[b], r_pp], writes=[r_Z[b]])

                def conv_post_b(bi):
                    t0, n = q_blocks[bi]; b = bi % 2
                    for oc in range(2):
                        ps, rp = nextG()
                        for ic in range(2):
                            mm(ps[:, 0:n], pwb[:, ic, oc * 128:(oc + 1) * 128], Z[b][:, ic, 0:n], ic == 0, ic == 1, [r_pw, r_Z[b]], [rp])
                        rc = [r_cat[6 + oc][i] for i in tiles(t0, n)]
                        op('dve', 'scalar_tensor_tensor', catT[:, 6 + oc, t0:t0 + n], ps[:, 0:n], pp[:, 37 + oc:38 + oc], catT[:, 6 + oc, t0:t0 + n],
                           ALU.add, ALU.mult, reads=[rp, r_pp] + rc, writes=rc)

                conv_dw(0)
                for bi in range(len(q_blocks)):
                    conv_post(bi)
                    if bi + 1 < len(q_blocks):
                        conv_dw(bi + 1)
                    conv_post_b(bi)
                dump('catconv%d' % l, catT[:, 6:8, :], r_cat[6] + r_cat[7])
                w1, rw1 = load_w(win_d, WOFF + 1024, WIN_EXT, 256)
                w2, rw2 = load_w(win_d, WOFF + 1280, WIN_EXT, 192)
                P.barrier()

            if stage < 4.2:
                continue
            with contextlib.ExitStack() as ph:
                pqn = sbt(ph, "pqn", [128, 2, T], BF16); r_pqn = [Res() for _ in range(NT)]
                pkvn = sbt(ph, "pkvn", [128, 1, T], BF16); r_pkvn = [Res() for _ in range(NT)]
                rt1 = [sbt(ph, "mrt1_0", [128, 512], F32)] * 2; r_rt1 = [Res()] * 2
                rt2 = [sbt(ph, "mrt2_0", [128, 512], F32)] * 2; r_rt2 = [Res()] * 2
                kT = [sbt(ph, "kT%d" % i, [128, T], BF16) for i in range(2)]; r_kT = [Res(), Res()]
                for i in range(2):
                    op('pool', 'memset', kT[i][:], 0.0, writes=[r_kT[i]])
                ph_a = contextlib.ExitStack()
                sq = [sbt(ph_a, "sq%d" % i, [128, 2, 512], BF16) for i in range(2)]; r_sq = [Res(), Res()]
                pqf = [sbt(ph_a, "pqf%d" % i, [128, 2, 512], F32) for i in range(2)]; r_pqf = [Res(), Res()]
                rr_ = [sbt(ph_a, "rr%d" % i, [128, 2, 512], F32) for i in range(2)]; r_rr = [Res(), Res()]
                items = [(w1, rw1, 2, pqn, r_pqn, blk, 1.0 / 256) for blk in q_blocks] + \
                        [(w2, rw2, 1, pkvn, r_pkvn, blk, 1.0 / 128) for blk in BLOCKS]

                def rms_x(ii):
                    w, rw, nch, dst, r_dst, (t0, n), inv_n = items[ii]; b = ii % 2
                    for cch in range(nch):
                        ps, rp = nextG()
                        for k in range(8):
                            mm(ps[:, 0:n], w[:, k, cch * 128:(cch + 1) * 128], uT[:, k, t0:t0 + n], k == 0, k == 7, [rw] + uTres(t0, n), [rp])
                        op('dve', 'tensor_copy', pqf[b][:, cch, 0:n], ps[:, 0:n], reads=[rp], writes=[r_pqf[b]])
                        op('dve', 'tensor_tensor', sq[b][:, cch, 0:n], pqf[b][:, cch, 0:n], pqf[b][:, cch, 0:n], ALU.mult, reads=[r_pqf[b]], writes=[r_sq[b]])

                def rms_y(ii):
                    w, rw, nch, dst, r_dst, (t0, n), inv_n = items[ii]; b = ii % 2
                    pss, rps = nextS()
                    for cch in range(nch):
                        mm(pss[:, 0:n], ones_bf[:], sq[b][:, cch, 0:n], cch == 0, cch == nch - 1, [r_ones, r_sq[b]], [rps])
                    op('act', 'activation', rr_[b][:, 0, 0:n], pss[:, 0:n], AF.Ln, bias=EPS_AP, scale=inv_n, reads=[rps, r_tab], writes=[r_rr[b]])
                    op('act', 'activation', rr_[b][:, 1, 0:n], rr_[b][:, 0, 0:n], AF.Exp, scale=-0.5, reads=[r_rr[b]], writes=[r_rr[b]])
                    for cch in range(nch):
                        op('dve', 'tensor_tensor', dst[:, cch, t0:t0 + n], pqf[b][:, cch, 0:n], rr_[b][:, 1, 0:n], ALU.mult,
                           reads=[r_pqf[b], r_rr[b]], writes=[r_dst[i] for i in tiles(t0, n)])

                def rope_key(j):
                    t0, n = BLOCKS[j]
                    ps, rp = nextG()
                    for k in range(8):
                        mm(ps[:, 0:n], w2[:, k, 64:192], uT[:, k, t0:t0 + n], k == 0, k == 7, [rw2] + uTres(t0, n), [rp])
                    op('dve', 'tensor_tensor', rt1[0][64:96, 0:n], ps[64:96, 0:n], RM[64:96, t0:t0 + n], ALU.mult, reads=[rp, r_RM], writes=[r_rt1[0]])
                    op('dve', 'tensor_tensor', rt2[0][64:96, 0:n], ps[96:128, 0:n], RM[96:128, t0:t0 + n], ALU.mult, reads=[rp, r_RM], writes=[r_rt2[0]])
                    op('pool', 'tensor_tensor', kT[0][64:96, t0:t0 + n], rt1[0][64:96, 0:n], rt2[0][64:96, 0:n], ALU.add,
                       reads=[r_rt1[0], r_rt2[0]], writes=[r_kT[0]])
                    op('dve', 'tensor_tensor', kT[1][64:96, t0:t0 + n], rt1[0][64:96, 0:n], rt2[0][64:96, 0:n], ALU.add,
                       reads=[r_rt1[0], r_rt2[0]], writes=[r_kT[1]])

                rms_x(0)
                for ii in range(len(items)):
                    if ii + 1 < len(items):
                        rms_x(ii + 1)
                    rms_y(ii)
                    if ii % 2 == 1 and ii // 2 < len(BLOCKS):
                        rope_key(ii // 2)
                for j in range(len(items) // 2, len(BLOCKS)):
                    rope_key(j)
                rrot = [0]
                dump('pqn%d' % l, pqn[:], r_pqn)
                dump('pkvn%d' % l, pkvn[:], r_pkvn)

                P.barrier()
                ph_a.close()
                if stage < 4.5:
                    continue
                qT = [sbt(ph, "qT%d" % i, [128, T], BF16) for i in range(2)]; r_qT = [Res(), Res()]
                va = [sbt(ph, "va%d" % i, [128, NT, 128], BF16) for i in range(2)]; r_va = [Res(), Res()]
                pT = [sbt(ph, "pT%d" % i, [128, 512], BF16) for i in range(4)]; r_pT = [Res() for _ in range(4)]
                rden = [sbt(ph, "rden%d" % i, [128, 512], F32) for i in range(2)]; r_rden = [Res(), Res()]
                ot2 = [sbt(ph, "ot2_%d" % i, [128, 512], F32) for i in range(2)]; r_ot2 = [Res(), Res()]
                for i in range(2):
                    op('pool', 'memset', qT[i][:], 0.0, writes=[r_qT[i]])
                    op('pool', 'memset', va[i][:], 1.0, writes=[r_va[i]])

                NB = len(BLOCKS)
                r_qn = [[Res() for _ in range(NB)] for _ in range(2)]
                r_qr = [[Res() for _ in range(NB)] for _ in range(2)]
                r_kn = [[Res() for _ in range(NB)] for _ in range(2)]
                r_vb = [[Res() for _ in range(NB)] for _ in range(2)]

                def piece(h, j, parts=(0, 1, 2)):
                    hb = h % 2
                    t0, n = BLOCKS[j]
                    if 0 in parts and (t0, n) in q_blocks:
                        ps, rp = nextG()
                        for k in range(2):
                            mm(ps[:, 0:n], wuqb[:, k, h * 128:(h + 1) * 128], pqn[:, k, t0:t0 + n], k == 0, k == 1,
                               [r_wuq] + [r_pqn[i] for i in tiles(t0, n)], [rp])
                        op('dve', 'tensor_copy', qT[hb][0:64, t0:t0 + n], ps[0:64, 0:n], reads=[rp, r_qT[hb]], writes=[r_qn[hb][j]])
                        op('dve', 'tensor_tensor', rt1[0][64:96, 0:n], ps[64:96, 0:n], RM[64:96, t0:t0 + n], ALU.mult, reads=[rp, r_RM], writes=[r_rt1[0]])
                        op('dve', 'tensor_tensor', rt2[0][64:96, 0:n], ps[96:128, 0:n], RM[96:128, t0:t0 + n], ALU.mult, reads=[rp, r_RM], writes=[r_rt2[0]])
                        op('pool', 'tensor_tensor', qT[hb][64:96, t0:t0 + n], rt1[0][64:96, 0:n], rt2[0][64:96, 0:n], ALU.add,
                           reads=[r_rt1[0], r_rt2[0], r_qT[hb]], writes=[r_qr[hb][j]])
                    if 1 in parts:
                        ps, rp = nextG()
                        mm(ps[0:64, 0:n], wukvb[:, h * 64:(h + 1) * 64], pkvn[:, 0, t0:t0 + n], True, True,
                           [r_wukv] + [r_pkvn[i] for i in tiles(t0, n)], [rp])
                        op('dve', 'tensor_copy', kT[hb][0:64, t0:t0 + n], ps[0:64, 0:n], reads=[rp, r_kT[hb]], writes=[r_kn[hb][j]])
                    if 2 in parts:
                        ps, rp = nextG()
                        i0 = t0 // 128; nb = n // 128
                        for ii in range(nb):
                            i = i0 + ii
                            mm(ps[:, ii * 64:(ii + 1) * 64], pkvn[:, 0, i * 128:(i + 1) * 128], wukvb[:, 512 + h * 64:512 + (h + 1) * 64], True, True,
                               [r_wukv, r_pkvn[i]], [rp])
                        op('dve', 'tensor_copy', va[hb][:, i0:i0 + nb, 0:64], ps[:, 0:nb * 64].rearrange("p (i e) -> p i e", e=64),
                           reads=[rp, r_va[hb]], writes=[r_vb[hb][j]])

                prot2 = [0]; orot = [0]
                SCALE = 96 ** -0.5

                def attend(h, inserts):
                    hb = h % 2
                    for bi, (t0, n) in enumerate(q_blocks):
                        jq = BLOCKS.index((t0, n))
                        kts = [0, 1] if t0 < CT else list(range(NT))
                        oi = orot[0]; orot[0] = 1 - oi
                        po = psA[:, oi * 512:oi * 512 + 512]; rpo = r_A[oi]
                        sps = {}

                        def issue_s(kt):
                            ps, rp = nextS()
                            jk = 0 if kt < 2 else 1 + (kt - 2) // 4
                            mm(ps[:, 0:n], kT[hb][:, kt * 128:(kt + 1) * 128], qT[hb][:, t0:t0 + n], True, True,
                               [r_kT[hb], r_kn[hb][jk], r_qn[hb][jq], r_qr[hb][jq]], [rp])
                            sps[kt] = (ps, rp)
                        issue_s(kts[0])
                        if len(kts) > 1:
                            issue_s(kts[1])
                        pend = list(inserts[bi])
                        for idx, kt in enumerate(kts):
                            if idx + 2 < len(kts):
                                issue_s(kts[idx + 2])
                            if pend and idx >= 2 and (idx - 2) % 3 == 0:
                                pend.pop(0)()
                            if idx == len(kts) - 1:
                                while pend:
                                    pend.pop(0)()
                            ps, rp = sps.pop(kt)
                            pi = prot2[0]; prot2[0] = (pi + 1) % 4
                            jk = 0 if kt < 2 else 1 + (kt - 2) // 4
                            op('act', 'activation', pT[pi][:, 0:n], ps[:, 0:n], AF.Exp, scale=SCALE, reads=[rp], writes=[r_pT[pi]])
                            mm(po[:, 0:n], va[hb][:, kt, :], pT[pi][:, 0:n], idx == 0, idx == len(kts) - 1, [r_va[hb], r_vb[hb][jk], r_pT[pi]], [rpo])
                        rb = oi
                        c = h // 2; rows = slice((h % 2) * 64, (h % 2) * 64 + 64)
                        op('dve', 'reciprocal', rden[rb][rows, 0:n], po[64:128, 0:n], reads=[rpo], writes=[r_rden[rb]])
                        op('dve', 'tensor_tensor', ot2[rb][rows, 0:n], po[0:64, 0:n], rden[rb][rows, 0:n], ALU.mult, reads=[rpo, r_rden[rb]], writes=[r_ot2[rb]])
                        rc = [r_cat[c][i] for i in tiles(t0, n)]
                        op('pool', 'tensor_tensor', catT[rows, c, t0:t0 + n], ot2[rb][rows, 0:n], catT[rows, c, t0:t0 + n], ALU.mult,
                           reads=[r_ot2[rb]] + rc, writes=rc)

                for j in range(NB):
                    piece(0, j)
                uflat = uT[:].rearrange("p k t -> p (k t)")
                woA = [uflat[:, j * 8192:(j + 1) * 8192].rearrange("p (k n) -> p k n", k=8) for j in range(2)]
                r_wo = Res('wo')
                nq = len(q_blocks)
                fold_pending = []
                for h in (range(8) if stage >= 5 else []):
                    inserts = [[] for _ in range(nq)]
                    if h + 1 < 8:
                        for j in range(NB):
                            slot = min(max(j - (NB - nq), 0), nq - 1)
                            for part_ in range(3):
                                inserts[slot].append(lambda h=h, j=j, part_=part_: piece(h + 1, j, (part_,)))
                    for fi_, f_ in enumerate(fold_pending):
                        inserts[fi_ % nq].append(f_)
                    fold_pending = []
                    attend(h, inserts)
                    if 1 <= h <= 4:
                        nh = h - 1
                        b = wrot[0]; wrot[0] = 1 - b
                        src = bass.AP(wout_d, l * D * D + nh * 256, [[D, 128], [128 * D, 8], [1, 256]])
                        dma(wst[b][:, :, :], src, writes=[r_wst[b]])
                        for j in (range(2) if need_ctx else range(1)):
                            for kk in range(4):
                                def fold_(b=b, j=j, kk=kk, nh=nh):
                                    op('dve', 'tensor_tensor', woA[j][:, 2 * kk:2 * kk + 2, nh * 256:(nh + 1) * 256], wst[b][:, 2 * kk:2 * kk + 2, :],
                                       bc(gate[:, j, nh * 256:(nh + 1) * 256], 1, 2), ALU.mult, reads=[r_wst[b], r_gate], writes=[r_wo])
                                fold_pending.append(fold_)
                dump('qT%d' % l, qT[0][:], [r_qT[0]])
                dump('kT%d' % l, kT[0][:], [r_kT[0]])
                dump('va%d' % l, va[0][:], [r_va[0]])
                dump('catmla%d' % l, catT[:, 0:4, :], r_cat[0] + r_cat[1] + r_cat[2] + r_cat[3])
                P.barrier()

            phm.close()

            if stage < 6:
                continue
            with contextlib.ExitStack() as ph:
                xin = [sbt(ph, "xin%d" % i, [128, D], F32) for i in range(3)]
                r_xin = [Res('xin%d' % i) for i in range(3)]
                lnp = sbt(ph, "lnp", [128, 2, D], F32); r_lnp = Res('lnp')
                dma(lnp[:, 0, :], bass.AP(lng_d, l * D, [[0, 128], [1, D]]), writes=[r_lnp])
                dma(lnp[:, 1, :], bass.AP(lnb_d, l * D, [[0, 128], [1, D]]), writes=[r_lnp])
                z = [sbt(ph, "z%d" % i, [128, D], F32)[:] for i in range(3)]; r_z = [Res(), Res(), Res()]
                z2 = [sbt(ph, "zn%d" % i, [128, D], F32)[:] for i in range(3)]; r_z2 = [Res(), Res(), Res()]
                pairs = [((psA[:, 0:512], r_A[0]), (psA[:, 512:1024], r_A[1])), ((psS[0], r_S[0]), (psS[1], r_S[1]))]
                if l == 0:
                    r_x1 = [Res('x1_%d' % i) for i in range(NT)]

                def p6_load(i):
                    if i >= RES0:
                        return
                    b = i % 3
                    dma(xin[b][:], src_d.ap()[i * 128:(i + 1) * 128, :], writes=[r_xin[b]])

                junk = sbt(ph, "junk", [128, D], BF16)[:]; r_junk = Res('junk')

                def p6_m(i):
                    pr = i % 2
                    j = 1 if i < 2 else 0
                    for nh in range(2):
                        ps, rp = pairs[pr][nh]
                        for k in range(8):
                            mm(ps[:, 0:512], catT[:, k, i * 128:(i + 1) * 128], woA[j][:, k, nh * 512:(nh + 1) * 512], k == 0, k == 7,
                               [r_cat[k][i], r_wo], [rp])
                    b = i % 3; zb = i % 3; s = i % 4
                    if i >= RES0:
                        xa, rx = xres[:, i - RES0, :], r_xres[i - RES0]
                    else:
                        xa, rx = xin[b][:], r_xin[b]
                    for nh in range(2):
                        ps, rp = pairs[pr][nh]
                        op('dve', 'scalar_tensor_tensor', z[zb][:, nh * 512:(nh + 1) * 512], xa[:, nh * 512:(nh + 1) * 512], ALPHA, ps[:, 0:512],
                           ALU.mult, ALU.add, accum_out=stat[:, s, nh:nh + 1], reads=[rx, rp], writes=[r_z[zb], r_stat[s]])
                    op('act', 'activation', junk, z[zb], AF.Square, accum_out=stat[:, s, 2:3], reads=[r_z[zb]], writes=[r_junk, r_stat[s]])

                def p6_n1(i):
                    s = i % 4
                    op('dve', 'tensor_tensor', stat[:, s, 3:4], stat[:, s, 0:1], stat[:, s, 1:2], ALU.add, reads=[r_stat[s]], writes=[r_stat[s]])
                    op('dve', 'tensor_scalar', stat[:, s, 3:4], stat[:, s, 3:4], 1.0 / D, None, ALU.mult, reads=[r_stat[s]], writes=[r_stat[s]])
                    op('dve', 'tensor_tensor', stat[:, s, 4:5], stat[:, s, 3:4], stat[:, s, 3:4], ALU.mult, reads=[r_stat[s]], writes=[r_stat[s]])
                    op('dve', 'scalar_tensor_tensor', stat[:, s, 5:6], stat[:, s, 2:3], 1.0 / D, stat[:, s, 4:5], ALU.mult, ALU.subtract,
                       reads=[r_stat[s]], writes=[r_stat[s]])
                    rstd_from_var(stat[:, s, 5:6], stat[:, s, 7:8], r_stat[s], stat[:, s, 6:7])

                def p6_n2(i):
                    s = i % 4
                    op('dve', 'scalar_tensor_tensor', stat[:, s, 8:9], stat[:, s, 3:4], -1.0, stat[:, s, 7:8], ALU.mult, ALU.mult,
                       reads=[r_stat[s]], writes=[r_stat[s]])
                    op('act', 'activation', z2[i % 3], z[i % 3], AF.Identity, bias=stat[:, s, 8:9], scale=stat[:, s, 7:8],
                       reads=[r_z[i % 3], r_stat[s]], writes=[r_z2[i % 3]])

                def p6_f(i):
                    zb = i % 3
                    op('dve', 'tensor_tensor', z2[zb], z2[zb], lnp[:, 0, :], ALU.mult, reads=[r_z2[zb], r_lnp], writes=[r_z2[zb]])
                    if i >= RES0 and l < n_layers - 1:
                        op('pool', 'tensor_tensor', xres[:, i - RES0, :], z2[zb], lnp[:, 1, :], ALU.add,
                           reads=[r_z2[zb], r_lnp], writes=[r_xres[i - RES0]])
                        return
                    op('pool', 'tensor_tensor', z2[zb], z2[zb], lnp[:, 1, :], ALU.add, reads=[r_z2[zb], r_lnp], writes=[r_z2[zb]])
                    if l == n_layers - 1 and i >= 2:
                        dma(out_d.ap()[(i - 2) * 128:(i - 1) * 128, :], z2[zb], reads=[r_z2[zb]], is_output=True)
                    else:
                        dma(x1_d.ap()[i * 128:(i + 1) * 128, :], z2[zb], reads=[r_z2[zb]], writes=[r_x1[i]], is_output=True)

                qt = q_tiles
                nq6 = len(qt)
                p6_stages = [(0, p6_load), (2, p6_m), (3, p6_n1), (4, p6_n2), (5, p6_f)]
                for t_ in range(nq6 + 5):
                    for lag_, fn_ in p6_stages:
                        ix_ = t_ - lag_
                        if 0 <= ix_ < nq6:
                            fn_(qt[ix_])
                if l + 1 < n_layers:
                    wmod_pref[l + 1] = [load_w32(wmod_d, (l + 1) * 1024 * 3072 + cb * 256, 3072, 256) for cb in range(2)]
                P.barrier()
        P.emit()
    return nc, dbg_outs


def _consts():
    tab = np.zeros((128, 6 * 128 + 2), np.float32)
    j = np.arange(128)[:, None].astype(np.float64); i = np.arange(128)[None, :].astype(np.float64)
    tab[:, 0:128] = np.maximum(i - j, 0)
    tab[:, 128:256] = (i >= j)
    tab[:, 256:384] = np.maximum(j - i, 0)
    tab[:, 384:512] = (j > i)
    tab[:, 512:640] = np.broadcast_to(i + 1, (128, 128))
    tab[:, 640:768] = np.broadcast_to(128 - i, (128, 128))
    tab[:, 768] = EPS
    tab[:, 769] = 1.0
    pos = np.arange(L)
    row = (pos // 64).astype(np.float64); col = (pos % 64).astype(np.float64)

    def cs(d2, p):
        inv = 10000.0 ** (-np.arange(0, d2, 2) / d2)
        ang = p[None, :] * inv[:, None]
        return np.cos(ang), np.sin(ang)
    cr, sr = cs(16, row); cc, sc = cs(16, col)
    C = np.concatenate([cr, cr, cc, cc], 0); S = np.concatenate([-sr, sr, -sc, sc], 0)
    ropeM = np.zeros((128, T), np.float32)
    ropeM[64:96, :CT] = 1.0
    ropeM[64:96, CT:] = C
    ropeM[96:128, CT:] = S
    cr, sr = cs(32, row); cc, sc = cs(32, col)
    C = np.concatenate([cr, cr, cc, cc], 0); S = np.concatenate([-sr, sr, -sc, sc], 0)
    ropeRC = np.ones((128, T), np.float32); ropeRS = np.zeros((128, T), np.float32)
    ropeRC[:, CT:] = np.concatenate([C, C], 0); ropeRS[:, CT:] = np.concatenate([S, S], 0)
    return tab, ropeM, ropeRC, ropeRS


def _prep_shared(inp):
    f = lambda a: np.ascontiguousarray(a, dtype=np.float32)
    w_in = inp['w_in']
    o = dict(pq=0, pkv=256, pkr=384, pg_mla=416, prq=928, prk=1184, prv=1440, pg_ret=1696, pglu=1952, pg_conv=2464)
    sw32 = np.concatenate([np.arange(8, 16), np.arange(0, 8), np.arange(24, 32), np.arange(16, 24)])
    sw64 = np.concatenate([np.arange(16, 32), np.arange(0, 16), np.arange(48, 64), np.arange(32, 48)])
    cols = []
    cols += list(range(o['pg_mla'], o['pg_mla'] + 512))
    cols += list(range(o['pg_ret'], o['pg_ret'] + 256))
    cols += list(range(o['pg_conv'], o['pg_conv'] + 256))
    cols += list(range(o['pq'], o['pq'] + 256))
    cols += list(range(o['pkv'], o['pkv'] + 128))
    cols += list(range(o['pkr'], o['pkr'] + 32))
    cols += list(o['pkr'] + sw32)
    for hp in range(2):
        for base in (o['prq'], o['prk']):
            cols += list(range(base + hp * 128, base + hp * 128 + 128))
            cols += [base + (hp * 2 + hh) * 64 + s for hh in range(2) for s in sw64]
    cols += list(range(o['prv'], o['prv'] + 256))
    cols += list(range(o['pglu'], o['pglu'] + 512))
    assert len(cols) == WIN_EXT
    w_in_ext = f(w_in[:, :, np.array(cols)])
    uq = inp['w_uq']
    ucols = []
    for h in range(8):
        ucols += list(range(h * 96, h * 96 + 96)) + list(h * 96 + 64 + sw32)
    w_uq_ext = f(uq[:, :, np.array(ucols)])
    kvc = [h * 128 + e for h in range(8) for e in range(64)] + [h * 128 + 64 + e for h in range(8) for e in range(64)]
    w_ukv_r = f(inp['w_ukv'][:, :, np.array(kvc)])
    pp = np.zeros((2, 128, NPP), np.float32)
    for l in range(2):
        pp[l, :, 0:24] = inp['b_mod'][l].reshape(24, 128).T
        pp[l, :, 24:26] = inp['mla_q_norm'][l].reshape(2, 128).T
        pp[l, :, 26] = inp['mla_kv_norm'][l]
        pp[l, :, 27:29] = inp['ret_gn_g'][l].reshape(2, 128).T
        pp[l, :, 29:31] = inp['ret_gn_b'][l].reshape(2, 128).T
        pp[l, :, 31:33] = inp['conv_dw_b'][l].reshape(2, 128).T
        pp[l, :, 33:35] = inp['conv_ln_g'][l].reshape(2, 128).T
        pp[l, :, 35:37] = inp['conv_ln_b'][l].reshape(2, 128).T
        pp[l, :, 37:39] = inp['conv_pw_b'][l].reshape(2, 128).T
        dw = inp['conv_dw'][l]
        pp[l, :, 39:101] = dw.T.reshape(2, 128, 31).transpose(1, 0, 2).reshape(128, 62)
    tab, ropeM, ropeRC, ropeRS = _consts()
    return dict(w_mod=f(inp['w_mod']), b_gate=f(inp['b_mod'][:, 2048:3072]), w_in=w_in_ext, w_uq=w_uq_ext, w_ukv=w_ukv_r,
                dec=f(np.concatenate([inp['ret_decay_fwd'], inp['ret_decay_bwd']], 1)), pw=f(inp['conv_pw']),
                w_out=f(inp['w_out']), ln_g=f(inp['ln_g']), ln_b=f(inp['ln_b']), pp=pp,
                ropeM=ropeM, ropeRC=ropeRC, ropeRS=ropeRS, tab=tab)


def _prep_core(inp, b):
    xin = np.ascontiguousarray(np.concatenate([inp['ctx'][b], inp['x'][b]], 0), dtype=np.float32)
    cc = np.stack([inp['c'][b].reshape(8, 128).T, inp['c_ctx'].reshape(8, 128).T], -1)
    return dict(xin=xin, cc=np.ascontiguousarray(cc.reshape(128, 16), dtype=np.float32))


_CACHE = {}


def kernel(**inputs):
    inp = {k: np.asarray(v) for k, v in inputs.items()}
    if 'nc' not in _CACHE:
        _CACHE['nc'] = build()[0]
    nc = _CACHE['nc']
    shared = _prep_shared(inp)
    in_maps = []
    for b in range(8):
        m = dict(shared)
        m.update(_prep_core(inp, b))
        in_maps.append(m)
    res = run_bass_kernel_spmd(nc, in_maps, core_ids=list(range(8)))
    out = np.stack([np.asarray(res.results[b]["out"]) for b in range(8)], 0)
    return out.astype(np.float32, copy=False)
```
